# Optimizing a Trainium2 kernel written in Bass

```python
import jax, jax.numpy as jnp
from jax import lax
import numpy as np

D_MODEL = 1024
BATCH = 32
SEQ = 256
DEPTH = 2
DEC_BATCH = 4
DEC_SEQ = 2048
PAST_LEN = 512

GRID_W = 64
EPS = 1e-6
CONV_W = 3
N_BRANCH = 4
D_A = 256
H_B = 4
DK_B = 64
DV_B = 64
DQK_B = H_B * DK_B
D_B = H_B * DV_B
HGRN_CHUNK = 32
H_C = 4
HD_C = 64
D_C = H_C * HD_C
NA_ROWS = 8
NA_COLS = 16
NA_QCB = NA_COLS
NA_KCB = 2 * NA_COLS
G_D = 4
D_D = 256
CHUNK_D = 128
D_FF = 2816

IN_SIZES = (D_A, D_A, D_A, DQK_B, DQK_B, DQK_B, D_B, D_B, D_C, D_C, D_C, D_D, D_D)
W_IN = sum(IN_SIZES)
IN_SPLITS = tuple(int(s) for s in np.cumsum(IN_SIZES)[:-1])

kernel_name = 'hybrid_gated_mixers_diffusion_step'


def rmsnorm(x, g):
    xf = x.astype(jnp.float32)
    y = xf * lax.rsqrt(jnp.mean(xf * xf, axis=-1, keepdims=True) + EPS)
    return (y * g.astype(jnp.float32)).astype(x.dtype)


def dwconv3(x, w, b):
    xp = jnp.pad(x, ((0, 0), (1, 1), (0, 0)))
    return xp[:, :-2] * w[0] + xp[:, 1:-1] * w[1] + xp[:, 2:] * w[2] + b


def _heads(t, n_heads):
    b, s, _ = t.shape
    return t.reshape(b, s, n_heads, -1).transpose(0, 2, 1, 3)


def _merge_heads(t):
    b, h, s, d = t.shape
    return t.transpose(0, 2, 1, 3).reshape(b, s, h * d)


def hgrn2_chunk_scan(q, k, log_f, v, s0):
    b_, h_, t_, _ = q.shape
    dv = v.shape[-1]
    nc = t_ // HGRN_CHUNK

    def chunks(a):
        return a.astype(jnp.float32).reshape(b_, h_, nc, HGRN_CHUNK, a.shape[-1])

    q, k, log_f, v = chunks(q), chunks(k), chunks(log_f), chunks(v)
    cum = jnp.cumsum(log_f, axis=3)
    tri = jnp.tril(jnp.ones((HGRN_CHUNK, HGRN_CHUNK), dtype=bool))[:, :, None]
    rel = cum[:, :, :, :, None, :] - cum[:, :, :, None, :, :]
    decay = jnp.exp(jnp.where(tri, rel, -jnp.inf))
    scores = jnp.einsum('bhctd,bhcsd,bhctsd->bhcts', q, k, decay)
    o_intra = jnp.einsum('bhcts,bhcsv->bhctv', scores, v)
    cum_end = cum[:, :, :, -1, :]
    u = jnp.einsum('bhcsd,bhcsv->bhcdv', k * jnp.exp(cum_end[:, :, :, None, :] - cum), v)
    g = jnp.exp(cum_end)

    def step(s, gu):
        g_c, u_c = gu
        return g_c[..., None] * s + u_c, s

    s_last, s_in = lax.scan(step, s0.astype(jnp.float32), (jnp.moveaxis(g, 2, 0), jnp.moveaxis(u, 2, 0)))
    s_in = jnp.moveaxis(s_in, 0, 2)
    o_inter = jnp.einsum('bhctd,bhcdv->bhctv', q * jnp.exp(cum), s_in)
    return (o_intra + o_inter).reshape(b_, h_, t_, dv), s_last


def hgrn2_bidir(q, z_f, z_b, i, g, lb_f, lb_b, norm_g, s0_f, s0_b):
    qh = _heads(jax.nn.silu(q), H_B)
    ih = _heads(i, H_B)

    def gates(z, lb):
        zf = z.astype(jnp.float32)
        lb = lb.astype(jnp.float32)
        log_f = jnp.log(lb + (1.0 - lb) * jax.nn.sigmoid(zf))
        k = (1.0 - lb) * jax.nn.sigmoid(-zf)
        return _heads(log_f, H_B), _heads(k, H_B)

    logf_f, k_f = gates(z_f, lb_f)
    logf_b, k_b = gates(z_b, lb_b)
    o_f, s_f = hgrn2_chunk_scan(qh, k_f, logf_f, ih, s0_f)

    def flip(t):
        return jnp.flip(t, axis=2)

    o_b, s_b = hgrn2_chunk_scan(flip(qh), flip(k_b), flip(logf_b), flip(ih), s0_b)
    o = o_f + flip(o_b)
    o = o * lax.rsqrt(jnp.mean(o * o, axis=-1, keepdims=True) + EPS)
    o = _merge_heads(o) * norm_g.astype(jnp.float32)
    return (o * jax.nn.silu(g.astype(jnp.float32))).astype(g.dtype), s_f, s_b


def context_attention(q, k, v):
    s = jnp.einsum('bhqd,bhkd->bhqk', q, k).astype(jnp.float32)
    p = jax.nn.softmax(s, axis=-1).astype(v.dtype)
    return jnp.einsum('bhqk,bhkd->bhqd', p, v)


def neighbourhood_attention(q, k, v, k_ctx, v_ctx, rpb):
    bsz, nh, t_len, hd = q.shape
    rows = t_len // GRID_W
    kr = min(NA_ROWS, rows)
    ncb = GRID_W // NA_QCB
    r_idx = jnp.arange(rows)
    key_rows = jnp.clip(r_idx - kr // 2, 0, rows - kr)[:, None] + jnp.arange(kr)[None, :]
    q_cols = jnp.arange(ncb)[:, None] * NA_QCB + jnp.arange(NA_QCB)[None, :]
    win_start = jnp.clip(q_cols - NA_COLS // 2, 0, GRID_W - NA_COLS)
    key_cols = (jnp.clip(jnp.arange(ncb) * NA_QCB - NA_COLS // 2, 0, GRID_W - NA_KCB)[:, None]
                + jnp.arange(NA_KCB)[None, :])
    kc = key_cols[:, None, :]
    valid = (kc >= win_start[:, :, None]) & (kc < win_start[:, :, None] + NA_COLS)
    d_row = key_rows - r_idx[:, None] + (NA_ROWS - 1)
    d_col = jnp.clip(kc - q_cols[:, :, None] + (NA_COLS - 1), 0, 2 * NA_COLS - 2)
    bias = rpb[:, d_row[:, None, None, :, None], d_col[None, :, :, None, :]].astype(jnp.float32)

    def gather(t):
        tg = t.reshape(bsz, nh, rows, GRID_W, hd)
        return jnp.take(jnp.take(tg, key_rows, axis=2), key_cols, axis=4)

    k_win, v_win = gather(k), gather(v)
    qg = q.reshape(bsz, nh, rows, ncb, NA_QCB, hd)
    s_win = jnp.einsum('bhrnqd,bhrjnkd->bhrnqjk', qg, k_win).astype(jnp.float32) + bias[None]
    s_win = jnp.where(valid[:, :, None, :], s_win, -jnp.inf)
    n_win = kr * NA_KCB
    s_ctx = jnp.einsum('bhrnqd,bhpd->bhrnqp', qg, k_ctx).astype(jnp.float32)
    s_all = jnp.concatenate([s_win.reshape(bsz, nh, rows, ncb, NA_QCB, n_win), s_ctx], axis=-1)
    p = jax.nn.softmax(s_all, axis=-1)
    p_win = p[..., :n_win].reshape(s_win.shape).astype(v.dtype)
    p_ctx = p[..., n_win:].astype(v.dtype)
    out = (jnp.einsum('bhrnqjk,bhrjnkd->bhrnqd', p_win, v_win)
           + jnp.einsum('bhrnqp,bhpd->bhrnqd', p_ctx, v_ctx))
    return out.reshape(bsz, nh, t_len, hd)


def chunk_gmlp(u, v, norm_g, ws, b):
    bsz, t_len, _ = v.shape
    nch = t_len // CHUNK_D
    vg = rmsnorm(v, norm_g).reshape(bsz, nch, CHUNK_D, G_D, D_D // G_D)
    mixed = jnp.einsum('gts,bnsgc->bntgc', ws, vg) + b.T[None, None, :, :, None]
    return u * mixed.reshape(bsz, t_len, D_D)


def conv_ffn(h, w_up, cw, cb, w_down):
    z = dwconv3(h @ w_up, cw, cb)
    a, g = jnp.split(z, 2, axis=-1)
    return (jax.nn.silu(g) * a) @ w_down


def token_mixers(h, lw, lb_f, lb_b, ctx_k=None, ctx_v=None, s0_f=None, s0_b=None):
    bsz = h.shape[0]
    proj = h @ lw['w_in']
    (a_x, a_b, a_c, b_q, b_ff, b_fb, b_i, b_g, c_q, c_k, c_v, d_u, d_v) = jnp.split(proj, IN_SPLITS, axis=-1)
    y_a = a_b * dwconv3(a_c * a_x, lw['conv_a_w'], lw['conv_a_b'])
    if s0_f is None:
        s0_f = jnp.zeros((bsz, H_B, DK_B, DV_B), jnp.float32)
        s0_b = jnp.zeros((bsz, H_B, DK_B, DV_B), jnp.float32)
    y_b, s_f, s_b = hgrn2_bidir(b_q, b_ff, b_fb, b_i, b_g, lb_f, lb_b, lw['hgrn_norm_g'], s0_f, s0_b)
    qh = _heads(c_q, H_C) * (HD_C ** -0.5)
    kh = _heads(c_k, H_C)
    vh = _heads(c_v, H_C)
    if ctx_k is None:
        y_c = _merge_heads(context_attention(qh, kh, vh))
    else:
        y_c = _merge_heads(neighbourhood_attention(qh, kh, vh, ctx_k, ctx_v, lw['na_rpb']))
    y_d = chunk_gmlp(d_u, d_v, lw['gmlp_norm_g'], lw['gmlp_ws'], lw['gmlp_b'])
    g_a, g_b, g_c, g_d = jnp.split(jax.nn.sigmoid(h @ lw['w_gate'] + lw['b_gate']), N_BRANCH, axis=-1)
    merged = (g_a * (y_a @ lw['w_br_a']) + g_b * (y_b @ lw['w_br_b'])
              + g_c * (y_c @ lw['w_br_c']) + g_d * (y_d @ lw['w_br_d']))
    return merged @ lw['w_o'], kh, vh, s_f, s_b


def trunk_layer(x, cvec, lw, lb_f, lb_b, ctx_k=None, ctx_v=None, s0_f=None, s0_b=None):
    mod = jax.nn.silu(cvec) @ lw['w_ada'] + lw['b_ada']
    if mod.ndim == 2:
        mod = mod[:, None, :]
    sh1, sc1, g1, sh2, sc2, g2 = jnp.split(mod, 6, axis=-1)
    h = rmsnorm(x, lw['norm1_g']) * (1.0 + sc1) + sh1
    mix, kh, vh, s_f, s_b = token_mixers(h, lw, lb_f, lb_b, ctx_k, ctx_v, s0_f, s0_b)
    x = x + g1 * mix
    h = rmsnorm(x, lw['norm2_g']) * (1.0 + sc2) + sh2
    x = x + g2 * conv_ffn(h, lw['w_up'], lw['ffn_conv_w'], lw['ffn_conv_b'], lw['w_down'])
    return x, kh, vh, s_f, s_b


def setup_inputs(seed: int = 0) -> dict:
    key = jax.random.key(seed)
    ks = jax.random.split(key, 32)
    f32 = jnp.float32
    D = D_MODEL

    def nrm(i, shape, scale):
        return jax.random.normal(ks[i], shape, f32) * scale

    return {
        'x_prompt': nrm(0, (BATCH, SEQ, D), 1.0),
        'x_sample': nrm(1, (DEC_BATCH, DEC_SEQ, D), 1.0),
        'cache_k': nrm(2, (DEC_BATCH, DEPTH, H_C, PAST_LEN, HD_C), 1.0),
        'cache_v': nrm(3, (DEC_BATCH, DEPTH, H_C, PAST_LEN, HD_C), 1.0),
        'state_hgrn': nrm(4, (DEC_BATCH, DEPTH, 2, H_B, DK_B, DV_B), 0.5),
        'c': nrm(5, (DEC_BATCH, D), 1.0),
        'c_ctx': nrm(6, (D,), 1.0),
        'norm1_g': 1.0 + nrm(7, (DEPTH, D), 0.05),
        'norm2_g': 1.0 + nrm(8, (DEPTH, D), 0.05),
        'w_ada': nrm(9, (DEPTH, D, 6 * D), 0.5 * D ** -0.5),
        'b_ada': nrm(10, (DEPTH, 6 * D), 0.01),
        'w_in': nrm(11, (DEPTH, D, W_IN), D ** -0.5),
        'conv_a_w': nrm(12, (DEPTH, CONV_W, D_A), CONV_W ** -0.5),
        'conv_a_b': nrm(13, (DEPTH, D_A), 0.01),
        'hgrn_lb': nrm(14, (DEPTH, 2, DQK_B), 0.5),
        'hgrn_norm_g': 1.0 + nrm(15, (DEPTH, D_B), 0.05),
        'na_rpb': nrm(16, (DEPTH, H_C, 2 * NA_ROWS - 1, 2 * NA_COLS - 1), 0.1),
        'gmlp_norm_g': 1.0 + nrm(17, (DEPTH, D_D), 0.05),
        'gmlp_ws': nrm(18, (DEPTH, G_D, CHUNK_D, CHUNK_D), CHUNK_D ** -0.5),
        'gmlp_b': 1.0 + nrm(19, (DEPTH, G_D, CHUNK_D), 0.05),
        'w_br_a': nrm(20, (DEPTH, D_A, D), D_A ** -0.5),
        'w_br_b': nrm(21, (DEPTH, D_B, D), D_B ** -0.5),
        'w_br_c': nrm(22, (DEPTH, D_C, D), D_C ** -0.5),
        'w_br_d': nrm(23, (DEPTH, D_D, D), D_D ** -0.5),
        'w_gate': nrm(24, (DEPTH, D, N_BRANCH * D), D ** -0.5),
        'b_gate': nrm(25, (DEPTH, N_BRANCH * D), 0.01),
        'w_o': nrm(26, (DEPTH, D, D), D ** -0.5),
        'w_up': nrm(27, (DEPTH, D, 2 * D_FF), D ** -0.5),
        'ffn_conv_w': nrm(28, (DEPTH, CONV_W, 2 * D_FF), CONV_W ** -0.5),
        'ffn_conv_b': nrm(29, (DEPTH, 2 * D_FF), 0.01),
        'w_down': nrm(30, (DEPTH, D_FF, D), D_FF ** -0.5),
        'final_norm_g': 1.0 + nrm(31, (D,), 0.05),
    }


def reference(x_prompt, x_sample, cache_k, cache_v, state_hgrn, c, c_ctx, norm1_g, norm2_g, w_ada, b_ada,
              w_in, conv_a_w, conv_a_b, hgrn_lb, hgrn_norm_g, na_rpb, gmlp_norm_g, gmlp_ws, gmlp_b,
              w_br_a, w_br_b, w_br_c, w_br_d, w_gate, b_gate, w_o, w_up, ffn_conv_w, ffn_conv_b, w_down,
              final_norm_g):
    lb = jnp.cumsum(jax.nn.softmax(hgrn_lb.astype(jnp.float32), axis=0), axis=0)
    lb = lb - lb[:1]
    xp, xs = x_prompt, x_sample
    ks, vs, ss = [], [], []
    for l in range(DEPTH):
        lw = {
            'norm1_g': norm1_g[l], 'norm2_g': norm2_g[l], 'w_ada': w_ada[l], 'b_ada': b_ada[l],
            'w_in': w_in[l], 'conv_a_w': conv_a_w[l], 'conv_a_b': conv_a_b[l],
            'hgrn_norm_g': hgrn_norm_g[l], 'na_rpb': na_rpb[l], 'gmlp_norm_g': gmlp_norm_g[l],
            'gmlp_ws': gmlp_ws[l], 'gmlp_b': gmlp_b[l], 'w_br_a': w_br_a[l], 'w_br_b': w_br_b[l],
            'w_br_c': w_br_c[l], 'w_br_d': w_br_d[l], 'w_gate': w_gate[l], 'b_gate': b_gate[l],
            'w_o': w_o[l], 'w_up': w_up[l], 'ffn_conv_w': ffn_conv_w[l], 'ffn_conv_b': ffn_conv_b[l],
            'w_down': w_down[l],
        }
        xp, kh, vh, s_f, s_b = trunk_layer(xp, c_ctx, lw, lb[l, 0], lb[l, 1])
        ks.append(kh)
        vs.append(vh)
        ss.append(jnp.stack([s_f, s_b], axis=1))
        xs, _, _, _, _ = trunk_layer(xs, c, lw, lb[l, 0], lb[l, 1], cache_k[:, l], cache_v[:, l],
                                     state_hgrn[:, l, 0], state_hgrn[:, l, 1])
    y_prompt = rmsnorm(xp, final_norm_g)
    y_sample = rmsnorm(xs, final_norm_g)
    new_cache_k = jnp.stack(ks, axis=1)
    new_cache_v = jnp.stack(vs, axis=1)
    new_state_hgrn = jnp.stack(ss, axis=1).astype(x_prompt.dtype)
    return (y_prompt, y_sample, new_cache_k, new_cache_v, new_state_hgrn)
```

```python
from contextlib import ExitStack
import numpy as np
import concourse.bass as bass
import concourse.mybir as mybir
from concourse.bass_utils import run_bass_kernel_spmd

F32 = mybir.dt.float32
BF16 = mybir.dt.bfloat16
AF = mybir.ActivationFunctionType
ALU = mybir.AluOpType
ENGS = ("pe", "act", "dve", "pool", "sp")
D = 1024
T = 2048
NSEG = T // 256
NT = T // 512
NT128 = T // 128
DFF = 2816
EPS = 1e-6
NEG = -30000.0
O_AX, O_AB, O_AC, O_BQ, O_BFF, O_BFB, O_BI, O_BG, O_CQ, O_CK, O_CV, O_DU, O_DV = (
    0, 256, 512, 768, 1024, 1280, 1536, 1792, 2048, 2304, 2560, 2816, 3072)
NCLS = 6


def cls_of(m):
    return {0: 0, 1: 1, 14: 4, 15: 5}.get(m, 2 + (m % 2))


def tile0_of(m):
    return min(max(m - 2, 0), NT128 - 5)


class Prog:
    def __init__(self):
        self.nc = bass.Bass("TRN2", target_bir_lowering=False)
        self.stack = ExitStack()
        self.streams = {e: [] for e in ENGS}
        self.cnt = {}
        self.lastw = {}
        self.readers = {}
        self.known = {e: {} for e in ENGS}
        self.out_sigs = []
        self.bank_i = 0
        self.pending = {}
        self.capture = None

    def barrier(self):
        snap = dict(self.cnt)
        for e in ENGS:
            p = self.pending.setdefault(e, {})
            for s_, v in snap.items():
                if self.known[e].get(s_, 0) < v and p.get(s_, 0) < v:
                    p[s_] = v

    def sb(self, name, shape, dt):
        return self.stack.enter_context(self.nc.sbuf_tensor(name, list(shape), dt))

    def ps(self, name, shape, dt=F32):
        return self.stack.enter_context(self.nc.psum_tensor(name, list(shape), dt))

    def dram(self, name, shape, dt, kind):
        return self.nc.dram_tensor(name, list(shape), dt, kind=kind).ap()

    def op(self, eng, emit, reads=(), writes=(), sem=None, out=False):
        if self.capture is not None:
            self.capture.append((eng, emit, tuple(reads), tuple(writes), sem, out))
            return None
        if sem is None:
            s = "E_" + eng
            inc = 1
        else:
            s = "D_" + sem
            inc = 16
        deps = []
        for k in reads:
            w = self.lastw.get(k)
            if w is not None:
                deps.append(w)
        for k in writes:
            w = self.lastw.get(k)
            if w is not None:
                deps.append(w)
            deps.extend(self.readers.get(k, ()))
        kn = self.known[eng]
        need = dict(self.pending.pop(eng, {}))
        for (ds, dv) in deps:
            if kn.get(ds, 0) >= dv:
                continue
            if eng == "pe" and ds == "E_pe":
                continue
            if need.get(ds, 0) < dv:
                need[ds] = dv
        for ds, dv in need.items():
            kn[ds] = dv
        self.cnt[s] = self.cnt.get(s, 0) + inc
        sig = (s, self.cnt[s])
        self.streams[eng].append((tuple(need.items()), emit, (s, inc)))
        for k in reads:
            self.readers.setdefault(k, []).append(sig)
        for k in writes:
            self.lastw[k] = sig
            self.readers[k] = []
        if out:
            self.out_sigs.append(sig)
        return sig

    def finish(self):
        nc = self.nc
        fin = {}
        for (s, v) in self.out_sigs:
            fin[s] = max(fin.get(s, 0), v)
        sems = {}
        for s in self.cnt:
            sems[s] = self.stack.enter_context(nc.semaphore(s))
        streams = self.streams

        def run(engname, eng):
            for (waits, emit, (s, inc)) in streams[engname]:
                for (ws, wv) in waits:
                    eng.wait_ge(sems[ws], wv)
                inst = emit(eng)
                inst.then_inc(sems[s], inc)
            if engname == "sp":
                for s, v in fin.items():
                    eng.wait_ge(sems[s], v)

        with nc.Block() as block:
            @block.tensor
            def _(e):
                run("pe", e)

            @block.scalar
            def _(e):
                run("act", e)

            @block.vector
            def _(e):
                run("dve", e)

            @block.gpsimd
            def _(e):
                run("pool", e)

            @block.sync
            def _(e):
                run("sp", e)
        self.stack.close()
        return nc


import os
HGL = int(os.environ.get("HGL", "9"))
WSLOT = 2048
NWS = 4


class Stop(Exception):
    pass


def build(plan=None, dbg=False, upto=99):
    P = Prog()
    nc = P.nc
    rec = []
    x_in = P.dram("x_in", [T, D], F32, "ExternalInput")
    vecs_d = P.dram("vecs", [128, NV], F32, "ExternalInput")
    cf_d = P.dram("c_f32", [128, 3 * 128 + 4], F32, "ExternalInput")
    cb_d = P.dram("c_bf", [128, 5 * 128], F32, "ExternalInput")
    gng_d = P.dram("gng_rep", [2, 128, 256], F32, "ExternalInput")
    gb_d = P.dram("gmlp_brow", [2, 1, 512], F32, "ExternalInput")
    wsT_d = P.dram("gmlp_wsT", [2, 128, 512], F32, "ExternalInput")
    s0_d = P.dram("s0", [2, 2, 2, 128, 128], F32, "ExternalInput")
    nab_d = P.dram("nabias", [2, 2, 128, NCLS * 2 * 5 * 128], F32, "ExternalInput")
    ctxk_d = P.dram("ctxk", [2, 512, 256], F32, "ExternalInput")
    ctxv_d = P.dram("ctxv", [2, 512, 256], F32, "ExternalInput")
    WD = {}
    for nm, shp in (("w_ada", [2, D, 6 * D]), ("w_in", [2, D, 3328]), ("w_gate", [2, D, 4 * D]),
                    ("w_br_a", [2, 256, D]), ("w_br_b", [2, 256, D]), ("w_br_c", [2, 256, D]),
                    ("w_br_d", [2, 256, D]), ("w_o", [2, D, D]), ("w_up", [2, D, 2 * DFF]),
                    ("w_down", [2, DFF, D])):
        WD[nm] = P.dram(nm, shp, F32, "ExternalInput")
    y_out = P.dram("y_out", [T, D], F32, "ExternalOutput")
    kc_out = P.dram("kc_out", [2, T, 256], F32, "ExternalOutput")
    vc_out = P.dram("vc_out", [2, T, 256], F32, "ExternalOutput")
    st_out = P.dram("st_out", [2, NSEG, 2, 2, 128, 128], F32, "ExternalOutput")
    if dbg:
        dbgY = P.dram("dbgY", [2, 128, 8 * T], BF16, "ExternalOutput")
        dbgX = P.dram("dbgX", [3, 128, 8 * T], F32, "ExternalOutput")

    X = P.sb("X", [128, 8, T], F32)
    H = P.sb("H", [128, 8, NSEG, 258], BF16)
    Y = P.sb("Y", [128, 8, T], BF16)
    AR = P.sb("AR", [128, 12 * T], BF16)
    WS = [P.sb("WS%d" % i, [128, WSLOT], BF16) for i in range(NWS)]
    STG = [P.sb("STG%d" % i, [128, 256], F32) for i in range(2)]
    vecs = P.sb("vecs_sb", [128, NV], F32)
    CF = P.sb("CF", [128, 3 * 128 + 4], F32)
    CB = P.sb("CBc", [128, 5 * 128], BF16)
    GNG = AR[:, 4 * T:4 * T + 512].bitcast(F32)
    WST = AR[:, 4 * T + 512:4 * T + 1024]
    GBR = AR[0:1, 4 * T + 1024:4 * T + 2048].bitcast(F32)
    ONEF = AR[0:1, 5 * T:5 * T + 256].bitcast(F32)
    modv = P.sb("modv", [128, 2 * 48], F32)
    lbv = P.sb("lbv", [128, 2, 2, 4], F32)
    ABv = P.sb("ABv", [128, 16], F32)
    SCb = P.sb("SCb", [128, 8], BF16)
    SQ = [P.sb("SQ0", [128, 512], BF16)] * 2
    RS = P.sb("RS", [128, 512], F32)
    RSTD = RS
    TMPF = [P.sb("TMPF0", [128, 512], F32)] * 2
    TMPB = [P.sb("TMPB%d" % i, [128, 512], BF16) for i in range(2)]
    SML = P.sb("SML", [128, 8], F32)
    SST = [P.sb("SST0", [128, 128], F32)] * 2
    SBF = [P.sb("SBF0", [128, 128], BF16)] * 2
    SSTG = [STG[1][:, 0:128]] * 2
    PSALL = P.ps("psall", [128, 8 * 512], F32)
    BANKS = [PSALL[:, i * 512:(i + 1) * 512] for i in range(8)]

    ident_f = CF[:, 0:128]
    blockmask = CF[:, 128:256]
    scanmask = CF[:, 256:384]
    cmask = CF[:, 384:388]
    ident_b = CB[:, 0:128]
    ones_b = CB[:, 128:256]
    blockones = CB[:, 256:384]
    trimask = [CB[:, 384:512], CB[:, 512:640]]

    def bank():
        i = P.bank_i % 8
        P.bank_i += 1
        return BANKS[i], ("bank", i)

    def plane(i, n=1, dt=BF16):
        a = AR[:, i * T:(i + n) * T]
        return a.bitcast(F32) if dt == F32 else a

    def vcol(c, n=1):
        return vecs[:, c:c + n]

    def ACT(out, in_, func, reads, writes, bias=None, scale=None, accum=None):
        def emit(e):
            kw = {}
            if bias is not None:
                kw["bias"] = bias
            if scale is not None:
                kw["scale"] = scale
            if accum is not None:
                kw["accum_out"] = accum
            return e.activation(out=out, in_=in_, func=func, **kw)
        P.op("act", emit, reads, writes)

    def TT(out, a, b, op, reads, writes, eng="dve"):
        P.op(eng, lambda e: e.tensor_tensor(out=out, in0=a, in1=b, op=op), reads, writes)

    def STT(out, in0, scalar, in1, op0, op1, reads, writes):
        P.op("dve", lambda e: e.scalar_tensor_tensor(out=out, in0=in0, scalar=scalar, in1=in1, op0=op0, op1=op1),
             reads, writes)

    def TS(out, in0, s1, s2, op0, op1, reads, writes, eng="dve"):
        if s2 is None:
            P.op(eng, lambda e: e.tensor_scalar(out=out, in0=in0, scalar1=s1, scalar2=None, op0=op0), reads, writes)
        else:
            P.op(eng, lambda e: e.tensor_scalar(out=out, in0=in0, scalar1=s1, scalar2=s2, op0=op0, op1=op1),
                 reads, writes)

    def MEMSET(ap, val, writes, eng="dve"):
        P.op(eng, lambda e: e.memset(ap, val), (), writes)

    def DMA(out, in_, reads, writes, sem, eng="sp", is_out=False):
        P.op(eng, lambda e: e.dma_start(out=out, in_=in_), reads, writes, sem=sem, out=is_out)

    def MM(items, reads, writes):
        def emit(e):
            inst = None
            for it in items:
                kw = {}
                if len(it) > 5 and it[5] is not None:
                    kw["tile_position"] = it[5]
                inst = e.matmul(it[0], lhsT=it[1], rhs=it[2], start=it[3], stop=it[4], **kw)
            return inst
        P.op("pe", emit, reads, writes)

    def TR(items, reads, writes):
        def emit(e):
            inst = None
            for (o, i, idn) in items:
                inst = e.transpose(out=o, in_=i, identity=idn)
            return inst
        P.op("pe", emit, reads, writes)

    wstate = {"i": 0, "issued": 0}

    def w_issue(desc, idx):
        (nm, l, r0, kc, c0, ncols) = desc
        slot = idx % NWS
        src = WD[nm][l, r0:r0 + kc * 128, c0:c0 + ncols].rearrange("(k p) n -> p k n", p=128)
        dst = WS[slot][:, 0:kc * ncols].rearrange("p (k n) -> p k n", k=kc)
        DMA(dst, src, (), [("w", slot)], sem="w%d" % slot, eng="pool")

    def wget(nm, l, r0, kc, c0, ncols):
        desc = (nm, l, r0, kc, c0, ncols)
        assert kc * ncols <= WSLOT
        rec.append(desc)
        i = wstate["i"]
        if plan is None:
            w_issue(desc, i)
        else:
            assert plan[i] == desc, (plan[i], desc)
            while wstate["issued"] < min(len(plan), i + NWS - 1):
                w_issue(plan[wstate["issued"]], wstate["issued"])
                wstate["issued"] += 1
        wstate["i"] = i + 1
        slot = i % NWS
        return WS[slot][:, 0:kc * ncols].rearrange("p (k n) -> p k n", k=kc), ("w", slot)

    DMA(vecs[:], vecs_d[:, :], (), ["vecs"], "vecs")
    DMA(CF[:], cf_d[:, :], (), ["CF"], "CF")
    DMA(CB[:], cb_d[:, :], (), ["CB"], "CB", eng="pool")
    CONST = ["vecs", "CF", "CB"]

    MEMSET(lbv[:, 0, 0, :], 0.0, [("lbv", 0)])
    MEMSET(lbv[:, 0, 1, :], 1.0, [("lbv", 0)])
    TT(lbv[:, 1, 0, :], vcol(V_HLB + 4, 4), vcol(V_HLB, 4), ALU.subtract, ["vecs"], [("lbv", 1)])
    ACT(lbv[:, 1, 0, :], lbv[:, 1, 0, :], AF.Sigmoid, [("lbv", 1)], [("lbv", 1)])
    TS(lbv[:, 1, 1, :], lbv[:, 1, 0, :], -1.0, 1.0, ALU.mult, ALU.add, [("lbv", 1)], [("lbv", 1)])

    XST = [AR[:, j * 2048:(j + 1) * 2048].bitcast(F32) for j in range(8)]
    for i in range(NT128):
        st, stk = XST[i % 8], ("xst", i % 8)
        DMA(st, x_in[i * 128:(i + 1) * 128, :], (), [stk], "xst%d" % (i % 8))
        for hf in range(2):
            bk, bkk = bank()
            TR([(bk[:, j * 128:(j + 1) * 128], st[:, (hf * 4 + j) * 128:(hf * 4 + j + 1) * 128], ident_f) for j in range(4)],
               [stk, "CF"], [bkk])
            wr = [("X", c, i // 4) for c in range(hf * 4, hf * 4 + 4)]
            if hf == 0:
                ACT(X[:, 0:4, i * 128:(i + 1) * 128], bk[:].rearrange("p (j t) -> p j t", j=4), AF.Copy, [bkk], wr)
            else:
                P.op("dve", lambda e, bk=bk, i=i: e.tensor_copy(out=X[:, 4:8, i * 128:(i + 1) * 128], in_=bk[:].rearrange("p (j t) -> p j t", j=4)), [bkk], wr)

    ACT(SCb[:], vcol(V_CVEC, 8), AF.Silu, ["vecs"], ["SCb"])

    def mod_group(l, g, fixed_bank=None):
        wv, wk = wget("w_ada", l, 0, 8, g * 256, 256)
        bk, bkk = bank() if fixed_bank is None else (BANKS[fixed_bank], ("bank", fixed_bank))
        items = []
        for jj in range(2):
            for k in range(8):
                items.append((bk[:, jj:jj + 1], wv[:, k, jj * 128:(jj + 1) * 128], SCb[:, k:k + 1], k == 0, k == 7))
        MM(items, [wk, "SCb"], [bkk])
        TT(modv[:, l * 48 + g * 2:l * 48 + g * 2 + 2], bk[:, 0:2], vcol(V_L[l] + VL_BADA + g * 2, 2), ALU.add,
           [bkk, "vecs"], [("modv", l, int(g >= 8))])
    for g in range(8):
        mod_group(0, g)
    deferred = [(0, g) for g in range(8, 24)] + [(1, g) for g in range(24)]

    def run_deferred(n=1, fixed_bank=None):
        for _ in range(n):
            if deferred:
                mod_group(*deferred.pop(0), fixed_bank=fixed_bank)

    def xkeys(c, t):
        return ("X", c, t)

    def stats(t, final=False):
        par = t % 2
        xk = [xkeys(c, t) for c in range(8)]
        if final:
            sq = AR[:, 16384 + par * 4096:16384 + (par + 1) * 4096].rearrange("p (c n) -> p c n", c=8)
            rs = [RS[:], TMPF[0][:]][par]
        else:
            sq = AR[:, par * 4096:(par + 1) * 4096].rearrange("p (c n) -> p c n", c=8)
            rs = AR[:, 8192 + par * 1024:8192 + (par + 1) * 1024].bitcast(F32)
        sqk, rsk = ("NSQ", par), ("NRS", par)
        ACT(sq, X[:, :, t * 512:(t + 1) * 512], AF.Square, xk, [sqk])
        bk, bkk = bank()
        MM([(bk[:], ones_b, sq[:, c, :], c == 0, c == 7) for c in range(8)], [sqk, "CB"], [bkk])
        ACT(rs, bk[:], AF.Sqrt, [bkk, "epsv"], [rsk], bias=SML[:, 0:1], scale=1.0 / D)
        P.op("dve", lambda e: e.reciprocal(out=rs, in_=rs), [rsk], [rsk])
        return rs, rsk

    MEMSET(SML[:, 0:1], EPS, ["epsv"])

    def norm_mod(l, which):
        gcol = V_L[l] + (VL_N1G if which == 0 else VL_N2G)
        sh = modv[:, l * 48 + which * 24:l * 48 + which * 24 + 8]
        sc = modv[:, l * 48 + which * 24 + 8:l * 48 + which * 24 + 16]
        STT(ABv[:, 0:8], sc, 1.0, vcol(gcol, 8), ALU.add, ALU.mult, [("modv", l, which), "vecs"], ["ABv"])
        TF = AR[:, 10240:10240 + 8192].bitcast(F32).rearrange("p (c n) -> p c n", c=8)
        for t in range(NT):
            rs, rsk = stats(t)
            TT(TF, X[:, :, t * 512:(t + 1) * 512], rs.unsqueeze(1).to_broadcast([128, 8, 512]), ALU.mult,
               [xkeys(c, t) for c in range(8)] + [rsk], ["NTF"])
            for c in range(8):
                ACT(H[:, c, 2 * t:2 * t + 2, 1:257], TF[:, c, :].rearrange("p (s n) -> p s n", s=2), AF.Identity,
                    ["NTF", "ABv", ("modv", l, which)], [("H", c, t)], bias=sh[:, c:c + 1], scale=ABv[:, c:c + 1])

    def h_rhs(k, t):
        return H[:, k, 2 * t:2 * t + 2, 1:257], [("H", k, t)]

    def lin_fm(nm, l, col0, nchunks, kc, rhs_fn, ntiles, evac, r0=0):
        oc = 0
        while oc < nchunks:
            ng = min(2, nchunks - oc)
            wv, wk = wget(nm, l, r0, kc, col0 + oc * 128, ng * 128)
            for jj in range(ng):
                for t in range(ntiles):
                    bk, bkk = bank()
                    items, rd = [], [wk]
                    for k in range(kc):
                        r, rk = rhs_fn(k, t)
                        items.append((bk[:], wv[:, k, jj * 128:(jj + 1) * 128], r, k == 0, k == kc - 1))
                        rd += rk
                    MM(items, rd, [bkk])
                    evac(oc + jj, t, bk, bkk)
            oc += ng

    def lin_tm(l, col0, ncols, evac):
        wv, wk = wget("w_in", l, 0, 8, col0, ncols)
        for i in range(NT128):
            seg, off = (i * 128) // 256, (i * 128) % 256
            bk, bkk = bank()
            items = [(bk[:, 0:ncols], H[:, k, seg, 1 + off:1 + off + 128], wv[:, k, :], k == 0, k == 7) for k in range(8)]
            MM(items, [wk] + [("H", k, i // 4) for k in range(8)], [bkk])
            evac(i, bk, bkk)

    def tsl(t):
        return slice(t * 512, (t + 1) * 512)

    def chk(l, stage):
        if l == 0 and upto <= stage:
            raise Stop()

    def layer(l):
        VL = V_L[l]
        P.barrier()
        chk(l, 1)
        norm_mod(l, 0)

        P.barrier()
        AXp, ABp, PXp, Zp = 0, 2, 4, 7
        PX = AR[:, PXp * T:PXp * T + 2 * NSEG * 258].rearrange("p (c s n) -> p c s n", c=2, s=NSEG)

        def evacA(oc, t, bk, bkk):
            c = oc % 2
            if oc < 2:
                ACT(plane(AXp + c)[:, tsl(t)], bk[:], AF.Copy, [bkk], [("AX", c, t)])
            elif oc < 4:
                ACT(plane(ABp + c)[:, tsl(t)], bk[:], AF.Copy, [bkk], [("ABp", c, t)])
            else:
                TT(PX[:, c, 2 * t:2 * t + 2, 1:257], bk[:].rearrange("p (s n) -> p s n", s=2),
                   plane(AXp + c)[:, tsl(t)].rearrange("p (s n) -> p s n", s=2), ALU.mult,
                   [bkk, ("AX", c, t)], [("PX", c, t)])
        lin_fm("w_in", l, O_AX, 6, 8, h_rhs, NT, evacA)
        for c in range(2):
            pk = [("PX", c, t) for t in range(NT)]
            MEMSET(PX[:, c, 0, 0:1], 0.0, [("PXh", c)])
            MEMSET(PX[:, c, NSEG - 1, 257:258], 0.0, [("PXh", c)])
            TS(PX[:, c, 1:NSEG, 0:1], PX[:, c, 0:NSEG - 1, 256:257], vcol(V_CARRY), None, ALU.mult, None,
               pk + ["vecs"], [("PXh", c)])
            TS(PX[:, c, 0:NSEG - 1, 257:258], PX[:, c, 1:NSEG, 1:2], vcol(V_CARRY), None, ALU.mult, None,
               pk + ["vecs"], [("PXh", c)])
            Z = plane(Zp + 2 * c, 2, F32).rearrange("p (s n) -> p s n", s=NSEG)
            zk = ("Z", c)
            allp = pk + [("PXh", c), "vecs"]
            TS(Z, PX[:, c, :, 1:257], vcol(VL + VL_CAW + 2 + c), vcol(VL + VL_CAB + c), ALU.mult, ALU.add, allp, [zk])
            STT(Z, PX[:, c, :, 0:256], vcol(VL + VL_CAW + 0 + c), Z, ALU.mult, ALU.add, allp + [zk], [zk])
            STT(Z, PX[:, c, :, 2:258], vcol(VL + VL_CAW + 4 + c), Z, ALU.mult, ALU.add, allp + [zk], [zk])
            TT(Y[:, 0 + c, :], plane(Zp + 2 * c, 2, F32), plane(ABp + c), ALU.mult,
               [zk] + [("ABp", c, t) for t in range(NT)], [("Y", 0 + c)])

        chk(l, 2)
        P.barrier()
        Up, VGp = 0, 2
        VG = AR[:, VGp * T:(VGp + 2) * T].rearrange("p (i n) -> p i n", i=NT128)
        DMA(GNG, gng_d[l, :, :], (), ["GNG"], "GNG")
        DMA(GBR, gb_d[l, :, :], (), ["GBR"], "GBR")
        DMA(WST, wsT_d[l, :, :], (), ["WST"], "WST", eng="pool")
        MEMSET(ONEF, 1.0, ["ONEF"])

        def evacDu(oc, t, bk, bkk):
            ACT(plane(Up + oc)[:, tsl(t)], bk[:], AF.Copy, [bkk], [("U", oc, t)])
        lin_fm("w_in", l, O_DU, 2, 8, h_rhs, NT, evacDu)

        def evacDv(i, bk, bkk):
            ACT(TMPB[0][:, 0:256], bk[:, 0:256], AF.Square, [bkk], [("TMPB", 0), ("SMLd", 0)], accum=SML[:, 1:2])
            ACT(SML[:, 2:3], SML[:, 1:2], AF.Sqrt, [("SMLd", 0), "epsv"], [("SMLd", 1)], bias=SML[:, 0:1], scale=1.0 / 256)
            P.op("dve", lambda e: e.reciprocal(out=SML[:, 3:4], in_=SML[:, 2:3]), [("SMLd", 1)], [("SMLd", 2)])
            STT(VG[:, i, :], bk[:, 0:256], SML[:, 3:4], GNG, ALU.mult, ALU.mult, [bkk, ("SMLd", 2), "GNG"], [("VG", i)])
        lin_tm(l, O_DV, 256, evacDv)
        for i in range(NT128):
            for dc in range(2):
                bk, bkk = bank()
                items = []
                for gg in range(2):
                    g = 2 * dc + gg
                    o = bk[:, gg * 128:(gg + 1) * 128]
                    items.append((o, VG[:, i, dc * 128:(dc + 1) * 128], WST[:, g * 128:(g + 1) * 128], True, False))
                    items.append((o, ONEF[0:1, :], GBR[0:1, g * 128:(g + 1) * 128], False, True))
                MM(items, [("VG", i), "WST", "GBR", "ONEF"], [bkk])
                for gg in range(2):
                    rs_ = slice(gg * 64, (gg + 1) * 64)
                    TT(Y[rs_, 6 + dc, i * 128:(i + 1) * 128], bk[rs_, gg * 128:(gg + 1) * 128],
                       plane(Up + dc)[rs_, i * 128:(i + 1) * 128], ALU.mult, [bkk, ("U", dc, i // 4)], [("Y", 6 + dc, i, gg)])

        chk(l, 3)
        P.barrier()
        NBT = NCLS * 2 * 5 * 128
        oVT, oQ, oK, oBT = 0, 4352, 6400, 8448
        oKC = oBT + NBT
        oVC = oKC + 1024
        oPT = 18432
        assert oVC + 1040 <= oPT
        VTA = AR[:, oVT:oVT + NT128 * 260].rearrange("p (i h n) -> p i h n", i=NT128, h=4)
        Qf, Kf = AR[:, oQ:oQ + T], AR[:, oK:oK + T]
        EBT = AR[:, oBT:oBT + NBT].rearrange("p (c h j q) -> p c h j q", c=NCLS, h=2, j=5)
        KCT = AR[:, oKC:oKC + 1024].rearrange("p (h n) -> p h n", h=2)
        VCXA = AR[:, oVC:oVC + 1040].rearrange("p (j h n) -> p j h n", j=4, h=4)
        PT = [AR[:, oPT + i * 1152:oPT + (i + 1) * 1152] for i in range(2)]
        REC = AR[:, oPT + 2304:oPT + 2304 + 8].bitcast(F32)
        YTK = [AR[:, oPT + 2368 + i * 128:oPT + 2368 + (i + 1) * 128] for i in range(2)]
        assert oPT + 2368 + 256 <= 12 * T
        MEMSET(VTA[:, :, :, 64:65], 1.0, ["VTA1"])
        MEMSET(VCXA[:, :, :, 64:65], 1.0, ["VCX1"])

        def evacK(i, bk, bkk):
            s = i % 2
            ACT(STG[s][:, 0:256], bk[:, 0:256], AF.Copy, [bkk], [("stg", s)])
            DMA(kc_out[l, i * 128:(i + 1) * 128, :], STG[s][:, 0:256], [("stg", s)], [], "stg%d" % s, is_out=True)
        lin_tm(l, O_CK, 256, evacK)

        def evacV(i, bk, bkk):
            s = i % 2
            ACT(STG[s][:, 0:256], bk[:, 0:256], AF.Copy, [bkk], [("stg", s)])
            DMA(vc_out[l, i * 128:(i + 1) * 128, :], STG[s][:, 0:256], [("stg", s)], [], "stg%d" % s, is_out=True)
            P.op("dve", lambda e: e.tensor_copy(out=VTA[:, i, :, 0:64], in_=bk[:, 0:256].rearrange("p (h n) -> p h n", h=4)),
                 [bkk, "VTA1"], [("VT", i)])
        lin_tm(l, O_CV, 256, evacV)
        for jt in range(4):
            DMA(VCXA[:, jt, :, 0:64], ctxv_d[l, jt * 128:(jt + 1) * 128, :].rearrange("p (h n) -> p h n", h=4), ["VCX1"], ["VCX"], "VCX", eng="pool")
        for jt in range(4):
            s = jt % 2
            DMA(STG[s][:, 0:256], ctxk_d[l, jt * 128:(jt + 1) * 128, :], (), [("stg", s)], "stg%d" % s)
            bk, bkk = bank()
            TR([(bk[:, hp * 128:(hp + 1) * 128], STG[s][:, hp * 128:(hp + 1) * 128], ident_f) for hp in range(2)],
               [("stg", s), "CF"], [bkk])
            ACT(KCT[:, :, jt * 128:(jt + 1) * 128], bk[:, 0:256].rearrange("p (h n) -> p h n", h=2), AF.Copy,
                [bkk], [("KCT", jt)])
        for hp in range(2):
            DMA(AR[:, oBT:oBT + NBT], nab_d[l, hp, :, :], (), ["BT"], "BT", eng="pool")
            ACT(AR[:, oBT:oBT + NBT], AR[:, oBT:oBT + NBT], AF.Exp, ["BT"], ["BT"])

            def evacQ(oc, t, bk, bkk):
                ACT(Qf[:, tsl(t)], bk[:], AF.Copy, [bkk], [("Q", t)], scale=0.125)

            def evacKf(oc, t, bk, bkk):
                ACT(Kf[:, tsl(t)], bk[:], AF.Copy, [bkk], [("K", t)])
            lin_fm("w_in", l, O_CQ + hp * 128, 1, 8, h_rhs, NT, evacQ)
            lin_fm("w_in", l, O_CK + hp * 128, 1, 8, h_rhs, NT, evacKf)
            def st1(it, hp=hp):
                m, hh = it // 2, it % 2
                t0, cls = tile0_of(m), cls_of(m)
                rs_ = slice(hh * 64, (hh + 1) * 64)
                pt, ptk = PT[it % 2], ("PT", it % 2)
                qv = Qf[rs_, m * 128:(m + 1) * 128]
                bks = [(BANKS[3 * (it % 2) + j], ("bank", 3 * (it % 2) + j)) for j in range(3)]
                for b3 in range(3):
                    bk, bkk = bks[b3]
                    items, rd = [], [("Q", m // 4)]
                    for jj in range(4 if b3 < 2 else 1):
                        o = bk[:, jj * 128:(jj + 1) * 128]
                        if b3 == 0:
                            lhs = KCT[rs_, hp, jj * 128:(jj + 1) * 128]
                            rd.append(("KCT", jj))
                        else:
                            kt = t0 + (b3 - 1) * 4 + jj
                            lhs = Kf[rs_, kt * 128:(kt + 1) * 128]
                            rd.append(("K", kt // 4))
                        items.append((o, lhs, qv, True, True))
                    MM(items, rd, [bkk])
                ACT(pt[:, 0:512], bks[0][0][:], AF.Exp, [bks[0][1], "vecs"], [ptk], bias=vcol(V_CTXB))
                ACT(pt[:, 512:1024], bks[1][0][:], AF.Exp, [bks[1][1]], [ptk])
                ACT(pt[:, 1024:1152], bks[2][0][:, 0:128], AF.Exp, [bks[2][1]], [ptk])
                TT(pt[:, 512:1152], pt[:, 512:1152], EBT[:, cls, hh, :, :].rearrange("p j q -> p (j q)"), ALU.mult,
                   [ptk, "BT"], [ptk])

            def st2(it, hp=hp):
                m, hh = it // 2, it % 2
                t0 = tile0_of(m)
                h = 2 * hp + hh
                pt, ptk = PT[it % 2], ("PT", it % 2)
                bo, bok = BANKS[6], ("bank", 6)
                items, rd = [], [ptk, "VCX", "VCX1", "VTA1"]
                for j in range(9):
                    if j < 4:
                        rhs = VCXA[:, j, h, :]
                    else:
                        rhs = VTA[:, t0 + j - 4, h, :]
                        rd.append(("VT", t0 + j - 4))
                    items.append((bo[:, hh * 65:hh * 65 + 65], pt[:, j * 128:(j + 1) * 128], rhs, j == 0, j == 8))
                MM(items, rd, [bok])
                if hh == 0:
                    return
                ytk = YTK[m % 2]
                ytkk = ("YTK", m % 2)
                P.op("dve", lambda e, bo=bo: e.reciprocal(out=REC[:, 0:2], in_=bo[:, 0:130].rearrange("p (h n) -> p h n", h=2)[:, :, 64]),
                     [bok], ["REC"])
                for h2 in range(2):
                    ACT(ytk[:, h2 * 64:(h2 + 1) * 64], bo[:, h2 * 65:h2 * 65 + 64], AF.Copy, [bok, "REC"], [ytkk], scale=REC[:, h2:h2 + 1])
                bt_, btk = BANKS[7], ("bank", 7)
                btb = bt_[:].bitcast(BF16)
                TR([(btb[:, 0:128], ytk, ident_b)], [ytkk, "CB"], [btk])
                P.op("dve", lambda e, btb=btb, m=m, hp=hp: e.tensor_copy(out=Y[:, 4 + hp, m * 128:(m + 1) * 128], in_=btb[:, 0:128]),
                     [btk], [("Y", 4 + hp, m, 0), ("Y", 4 + hp, m, 1)])

            NIT = 2 * NT128
            st1(0)
            for it in range(NIT):
                if it + 1 < NIT:
                    st1(it + 1)
                st2(it)

        chk(l, 4)
        P.barrier()
        LFp, LBp, KFp, KBp, QSp, SGp, VIp, OAp, TMp = 0, 2, 4, 5, 6, 7, 8, 9, 11
        tmb = TMp * T

        def tmpb(i):
            return AR[:, tmb + i * 128:tmb + (i + 1) * 128]

        def tmpf(i):
            return AR[:, tmb + 1024 + i * 256:tmb + 1024 + (i + 1) * 256].bitcast(F32)
        for hp in range(2):
            P.barrier()
            LG = [plane(LFp, 2, F32), plane(LBp, 2, F32)]
            KG = [plane(KFp), plane(KBp)]
            QS, SG, OA = plane(QSp), plane(SGp), plane(OAp, 2, F32)
            VI = AR[:, VIp * T:(VIp + 1) * T].rearrange("p (i n) -> p i n", i=NT128)

            def evacBq(oc, t, bk, bkk):
                ACT(QS[:, tsl(t)], bk[:], AF.Silu, [bkk], [("QS", t)])

            def mk_evacF(d):
                def ev(oc, t, bk, bkk):
                    lg = LG[d][:, tsl(t)]
                    ACT(lg, bk[:], AF.Sigmoid, [bkk], [("LG", d, t)])
                    TS(lg, lg, lbv[:, l, 1, d * 2 + hp:d * 2 + hp + 1], lbv[:, l, 0, d * 2 + hp:d * 2 + hp + 1], ALU.mult, ALU.add,
                       [("LG", d, t), ("lbv", l)], [("LG", d, t)])
                    TS(KG[d][:, tsl(t)], lg, -1.0, 1.0, ALU.mult, ALU.add, [("LG", d, t)], [("KG", d, t)])
                    ACT(lg, lg, AF.Ln, [("LG", d, t)], [("LG", d, t)])
                return ev

            def evacBg(oc, t, bk, bkk):
                ACT(SG[:, tsl(t)], bk[:], AF.Silu, [bkk], [("SG", t)])

            def evacBi(i, bk, bkk):
                P.op("dve", lambda e: e.tensor_copy(out=VI[:, i, :], in_=bk[:, 0:128]), [bkk], [("VI", i)])
            lin_fm("w_in", l, O_BQ + hp * 128, 1, 8, h_rhs, NT, evacBq)
            lin_fm("w_in", l, O_BFF + hp * 128, 1, 8, h_rhs, NT, mk_evacF(0))
            lin_fm("w_in", l, O_BFB + hp * 128, 1, 8, h_rhs, NT, mk_evacF(1))
            lin_fm("w_in", l, O_BG + hp * 128, 1, 8, h_rhs, NT, evacBg)
            lin_tm(l, O_BI + hp * 128, 128, evacBi)
            stg0b = STG[0][:].bitcast(BF16)
            TSETS = [
                dict(CUM=tmpf(0), E1=tmpf(1), EX=tmpf(2), D3=tmpf(3),
                     QT2=AR[:, tmb:tmb + 256], KT=tmpb(2), X32=AR[:, tmb + 384:tmb + 640], AM=AR[:, tmb + 640:tmb + 896],
                     KHT=tmpb(7), VIB=TMPB[0][:, 0:512]),
                dict(CUM=TMPF[0][:, 0:128], E1=TMPF[0][:, 128:256], EX=TMPF[0][:, 256:384], D3=TMPF[0][:, 384:512],
                     QT2=TMPB[1][:, 0:256], KT=TMPB[1][:, 256:384], X32=stg0b[:, 0:256], AM=stg0b[:, 256:512],
                     KHT=TMPB[1][:, 384:512], VIB=SQ[0][:, 0:512]),
            ]
            SS = [SST[0][:], RS[:, 0:128]]
            SBs = [RS[:, 128:384].bitcast(BF16)[:, j * 128:(j + 1) * 128] for j in range(4)]
            HS = [slice(0, 64), slice(64, 128)]
            P.barrier()
            for d in (range(2) if HGL >= 2 else ()):
                kc = 0
                DMA(SS[0], s0_d[l, d, hp, :, :], (), [("S", 0)], "S0")
                tiles = range(NT128) if d == 0 else range(NT128 - 1, -1, -1)
                for ts_ in range(2):
                    MEMSET(TSETS[ts_]["QT2"], 0.0, [("QT", ts_)])
                    MEMSET(TSETS[ts_]["X32"], 0.0, [("X3", ts_)])
                def prep(i, ts_, d=d):
                    tm = TSETS[ts_]
                    K_ = lambda n: (n, ts_)
                    cs = slice(i * 128, (i + 1) * 128)
                    tq = i // 4
                    CUM, E1, EX, D3 = tm["CUM"], tm["E1"], tm["EX"], tm["D3"]
                    QT2 = tm["QT2"].rearrange("p (h n) -> p h n", h=2)
                    X32 = tm["X32"].rearrange("p (h n) -> p h n", h=2)
                    X3 = tm["X32"][:, 0:128]
                    KT, AM, KHT, VIB = tm["KT"], tm["AM"], tm["KHT"], tm["VIB"]
                    P.op("dve", lambda e, CUM=CUM, cs=cs, d=d: e.tensor_tensor_scan(
                        out=CUM, data0=scanmask, data1=LG[d][:, cs], initial=0.0, op0=ALU.mult, op1=ALU.add),
                        [("LG", d, tq), "CF"], [K_("CUM")])
                    CUM3 = CUM.rearrange("p (c j) -> p c j", j=32)
                    TOTB = CUM3[:, :, 31:32].to_broadcast([128, 4, 32])
                    ACT(E1, CUM, AF.Exp, [K_("CUM")], [K_("E1")])
                    if d == 0:
                        for hh in range(2):
                            TT(QT2[HS[hh], hh, :], QS[HS[hh], cs], E1[HS[hh], :], ALU.mult, [("QS", tq), K_("E1")], [K_("QT")])
                        ACT(EX, CUM, AF.Exp, [K_("CUM")], [K_("EX")], scale=-1.0)
                        TT(KT, KG[0][:, cs], EX, ALU.mult, [("KG", 0, tq), K_("EX")], [K_("KT")])
                        TT(D3.rearrange("p (c j) -> p c j", j=32), TOTB, CUM3, ALU.subtract, [K_("CUM")], [K_("D3")])
                        ACT(D3, D3, AF.Exp, [K_("D3")], [K_("D3")])
                        TT(X3, KG[0][:, cs], D3, ALU.mult, [("KG", 0, tq), K_("D3")], [K_("X3")])
                        qint, qintk, ktr, ktrk = QT2, K_("QT"), X3, K_("X3")
                    else:
                        TT(D3.rearrange("p (c j) -> p c j", j=32), TOTB, CUM3, ALU.subtract, [K_("CUM")], [K_("D3")])
                        TT(D3, D3, LG[1][:, cs], ALU.add, [K_("D3"), ("LG", 1, tq)], [K_("D3")])
                        TT(CUM, CUM, LG[1][:, cs], ALU.subtract, [K_("CUM"), ("LG", 1, tq)], [K_("CUM")])
                        ACT(EX, CUM, AF.Exp, [K_("CUM")], [K_("EX")], scale=-1.0)
                        for hh in range(2):
                            TT(QT2[HS[hh], hh, :], QS[HS[hh], cs], EX[HS[hh], :], ALU.mult, [("QS", tq), K_("EX")], [K_("QT")])
                        ACT(EX, CUM, AF.Exp, [K_("CUM")], [K_("EX")])
                        TT(KT, KG[1][:, cs], EX, ALU.mult, [("KG", 1, tq), K_("EX")], [K_("KT")])
                        ACT(D3, D3, AF.Exp, [K_("D3")], [K_("D3")])
                        for hh in range(2):
                            TT(X32[HS[hh], hh, :], QS[HS[hh], cs], D3[HS[hh], :], ALU.mult, [("QS", tq), K_("D3")], [K_("X3")])
                        qint, qintk, ktr, ktrk = X32, K_("X3"), KT, K_("KT")
                    ba, bak = BANKS[6], ("bank", 6)
                    MM([(ba[:, hh * 128:(hh + 1) * 128], KT, QT2[:, hh, :], True, True)
                        for hh in range(2)], [K_("KT"), K_("QT")], [bak])
                    TT(AM.rearrange("p (h n) -> p h n", h=2), ba[:, 0:256].rearrange("p (h n) -> p h n", h=2),
                       trimask[d].unsqueeze(1).to_broadcast([128, 2, 128]), ALU.mult, [bak, "CB"], [K_("AM")])
                    bt_, btk = BANKS[7], ("bank", 7)
                    btb = bt_[:].bitcast(BF16)
                    TR([(btb[:, 0:128], ktr, ident_b)], [ktrk, "CB"], [btk])
                    ACT(KHT, btb[:, 0:128], AF.Copy, [btk], [K_("KHT")])
                    bu, buk = BANKS[3 * ts_], ("bank", 3 * ts_)
                    for c4 in range(4):
                        ACT(VIB[:, c4 * 128:(c4 + 1) * 128], VI[:, i, :], AF.Copy, [("VI", i), "CF"], [K_("VIB")], scale=cmask[:, c4:c4 + 1])
                    MM([(bu[:, 0:512], KHT, VIB, True, True)], [K_("KHT"), K_("VIB")], [buk])
                    bo = [(BANKS[3 * ts_ + 1 + hh], ("bank", 3 * ts_ + 1 + hh)) for hh in range(2)]
                    for hh in range(2):
                        MM([(bo[hh][0][:, 0:128], VI[:, i, :], AM[:, hh * 128:(hh + 1) * 128], True, False)],
                           [("VI", i), K_("AM")], [bo[hh][1]])
                    return dict(i=i, cs=cs, E1=E1, e1k=K_("E1"), qint=qint, qintk=qintk, bu=bu, buk=buk, bo=bo)

                def chain(cx, d=d):
                    nonlocal kc
                    i, cs, E1, qint, qintk, bu, buk, bo = (cx[k] for k in ("i", "cs", "E1", "qint", "qintk", "bu", "buk", "bo"))
                    chunks = range(4) if d == 0 else range(3, -1, -1)
                    for c in chunks:
                        Sc, Sn = SS[kc % 2], SS[(kc + 1) % 2]
                        sck, snk = ("S", kc % 2), ("S", (kc + 1) % 2)
                        Sb, sbk = SBs[kc % 4], ("Sb", kc % 4)
                        kc += 1
                        ACT(Sb, Sc, AF.Copy, [sck], [sbk])
                        last = (c == 3) if d == 0 else (c == 0)
                        for hh in range(2):
                            MM([(bo[hh][0][:, 32 * c:32 * c + 32], Sb, qint[:, hh, 32 * c:32 * c + 32], False, last)],
                               [sbk, qintk], [bo[hh][1]])
                        STT(Sn, Sc, E1[:, 32 * c + 31:32 * c + 32], bu[:, c * 128:(c + 1) * 128], ALU.mult, ALU.add,
                            [sck, cx["e1k"], buk], [snk])
                        tokpos = i * 128 + 32 * c
                        at_end = ((tokpos + 32) % 256 == 0) if d == 0 else (tokpos % 256 == 0)
                        if at_end:
                            seg = tokpos // 256
                            sg_ = SSTG[d]
                            ACT(sg_, Sn, AF.Copy, [snk], [("SSTG", 0)])
                            DMA(st_out[l, seg, d, hp, :, :], sg_, [("SSTG", 0)], [], "sstg0", is_out=True)
                            TS(Sn, Sn, vcol(V_CARRY), None, ALU.mult, None, [snk, "vecs"], [snk])
                    run_deferred(1, fixed_bank=7)
                    for hh in range(2):
                        rs_ = slice(hh * 64, (hh + 1) * 64)
                        if d == 0:
                            ACT(OA[rs_, cs], bo[hh][0][rs_, 0:128], AF.Copy, [bo[hh][1]], [("OA", i, hh)])
                        else:
                            TT(OA[rs_, cs], bo[hh][0][rs_, 0:128], OA[rs_, cs], ALU.add, [bo[hh][1], ("OA", i, hh)], [("OA", i, hh)])

                tl = list(tiles)
                cxs = {0: prep(tl[0], 0)}
                for n in range(len(tl)):
                    la, lb = [], []
                    if n + 1 < len(tl):
                        P.capture = la
                        cxs[n + 1] = prep(tl[n + 1], (n + 1) % 2)
                    P.capture = lb
                    chain(cxs.pop(n))
                    P.capture = None
                    ia = ib = 0
                    while ia < len(la) or ib < len(lb):
                        if ib < len(lb):
                            P.op(*lb[ib])
                            ib += 1
                        if ia < len(la):
                            P.op(*la[ia])
                            ia += 1
                P.barrier()
            for t in (range(NT) if HGL >= 6 else ()):
                oak = [("OA", i, hh) for i in range(4 * t, 4 * t + 4) for hh in range(2)]
                ACT(SQ[0][:], OA[:, tsl(t)], AF.Square, oak, [("SQ", 0)])
                bk, bkk = bank()
                MM([(bk[:], blockones, SQ[0][:], True, True)], [("SQ", 0), "CB"], [bkk])
                ACT(RS[:], bk[:], AF.Sqrt, [bkk, "epsv"], ["RS", "RSTD"], bias=SML[:, 0:1], scale=1.0 / 64)
                P.op("dve", lambda e: e.reciprocal(out=RSTD[:], in_=RS[:]), ["RS"], ["RSTD", "RS"])
                TT(TMPF[0][:], OA[:, tsl(t)], RSTD[:], ALU.mult, oak + ["RSTD"], [("TMPF", 0)])
                STT(Y[:, 2 + hp, tsl(t)], TMPF[0][:], vcol(VL + VL_HNG + hp), SG[:, tsl(t)], ALU.mult, ALU.mult,
                    [("TMPF", 0), "vecs", ("SG", t)], [("Y", 2 + hp, t)])

        if dbg:
            DMA(dbgY[l, :, :], Y[:].rearrange("p c t -> p (c t)"),
                [("Y", c) for c in range(2)] + [("Y", 6 + dc, i, gg) for dc in range(2) for i in range(NT128) for gg in range(2)]
                + [("Y", 4 + hp, m, hh) for hp in range(2) for m in range(NT128) for hh in range(2)]
                + [("Y", 2 + hp, t) for hp in range(2) for t in range(NT)], [], "dbgY", is_out=True)

        run_deferred(100)
        chk(l, 6)
        P.barrier()
        MGp, MAp = 0, 8
        MG = AR[:, MGp * T:(MGp + 8) * T].rearrange("p (c t) -> p c t", c=8)
        MACC = AR[:, MAp * T:(MAp + 4) * T].bitcast(F32).rearrange("p (c t) -> p c t", c=2)
        brn = ["w_br_a", "w_br_b", "w_br_c", "w_br_d"]

        def ykeys(br, k, t):
            c = 2 * br + k
            if br == 0:
                return [("Y", c)]
            if br == 1:
                return [("Y", c, t)]
            return [("Y", c, i, gg) for i in range(4 * t, 4 * t + 4) for gg in range(2)]
        for jp in range(4):
            for br in range(4):
                gw, gwk = wget("w_gate", l, 0, 8, br * 1024 + jp * 256, 256)
                bw, bwk = wget(brn[br], l, 0, 2, jp * 256, 256)
                for jj in range(2):
                    j = jp * 2 + jj
                    for t in range(NT):
                        b1, b1k = bank()
                        items, rd = [], [gwk]
                        for k in range(8):
                            r, rk = h_rhs(k, t)
                            items.append((b1[:], gw[:, k, jj * 128:(jj + 1) * 128], r, k == 0, k == 7))
                            rd += rk
                        MM(items, rd, [b1k])
                        gt = TMPB[(t + jj) % 2]
                        gtk = ("TMPB", (t + jj) % 2)
                        ACT(gt[:], b1[:], AF.Sigmoid, [b1k, "vecs"], [gtk], bias=vcol(VL + VL_BGATE + br * 8 + j))
                        b2, b2k = bank()
                        MM([(b2[:], bw[:, k, jj * 128:(jj + 1) * 128], Y[:, 2 * br + k, tsl(t)], k == 0, k == 1) for k in range(2)],
                           [bwk] + ykeys(br, 0, t) + ykeys(br, 1, t), [b2k])
                        mk = ("MACC", jj, t)
                        if br == 0:
                            TT(MACC[:, jj, tsl(t)], b2[:], gt[:], ALU.mult, [b2k, gtk], [mk])
                        else:
                            tf = TMPF[(t + jj) % 2]
                            tfk = ("TMPF", 0)
                            TT(tf[:], b2[:], gt[:], ALU.mult, [b2k, gtk], [tfk])
                            if br < 3:
                                TT(MACC[:, jj, tsl(t)], MACC[:, jj, tsl(t)], tf[:], ALU.add, [mk, tfk], [mk], eng="pool")
                            else:
                                TT(MG[:, j, tsl(t)], MACC[:, jj, tsl(t)], tf[:], ALU.add, [mk, tfk], [("MG", j, t)], eng="pool")

        def mg_rhs(k, t):
            return MG[:, k, tsl(t)], [("MG", k, t)]

        def evacO(oc, t, bk, bkk):
            STT(X[:, oc, tsl(t)], bk[:], modv[:, l * 48 + 16 + oc:l * 48 + 17 + oc], X[:, oc, tsl(t)], ALU.mult, ALU.add,
                [bkk, ("modv", l, 1), xkeys(oc, t)], [xkeys(oc, t)])
        lin_fm("w_o", l, 0, 8, 8, mg_rhs, NT, evacO)
        if dbg:
            DMA(dbgX[l, :, :], X[:].rearrange("p c t -> p (c t)"), [xkeys(c, t) for c in range(8) for t in range(NT)], [], "dbgX", is_out=True)

        chk(l, 7)
        P.barrier()
        norm_mod(l, 1)
        for c in range(8):
            hk = [("H", c, t) for t in range(NT)]
            MEMSET(H[:, c, 0, 0:1], 0.0, [("Hh", c)])
            MEMSET(H[:, c, NSEG - 1, 257:258], 0.0, [("Hh", c)])
            TS(H[:, c, 1:NSEG, 0:1], H[:, c, 0:NSEG - 1, 256:257], vcol(V_CARRY), None, ALU.mult, None, hk + ["vecs"], [("Hh", c)])
            TS(H[:, c, 0:NSEG - 1, 257:258], H[:, c, 1:NSEG, 1:2], vcol(V_CARRY), None, ALU.mult, None, hk + ["vecs"], [("Hh", c)])
        AV = AR[:, 0:11 * T].rearrange("p (c t) -> p c t", c=22)
        ZA = [AR[:, 11 * T + i * 1024:11 * T + (i + 1) * 1024].bitcast(F32).rearrange("p (s n) -> p s n", s=2) for i in range(2)]
        ZG = [TMPF[0][:].rearrange("p (s n) -> p s n", s=2), RS[:].rearrange("p (s n) -> p s n", s=2)]
        P.barrier()
        itf = 0
        for hf in range(2):
            for pi in range(22):
                wa, wak = wget("w_up", l, 0, 8, pi * 128, 128)
                wg, wgk = wget("w_up", l, 0, 8, DFF + pi * 128, 128)
                for s2 in range(2):
                    par = itf % 2
                    itf += 1
                    res = []
                    for (wv, wk, ch, zi) in ((wa, wak, pi, 0), (wg, wgk, 22 + pi, 1)):
                        b0 = 4 * par + 2 * zi
                        bkeys = [("bank", b0), ("bank", b0 + 1)]
                        for j in range(2):
                            seg = hf * 4 + s2 * 2 + j
                            rd = [("H", k, seg // 2) for k in range(8)] + [("Hh", k) for k in range(8)]
                            MM([(BANKS[b0 + j][:, 0:258], wv[:, k, :], H[:, k, seg, :], k == 0, k == 7) for k in range(8)], rd + [wk], [bkeys[j]])
                        PS2 = PSALL[:, b0 * 512:(b0 + 2) * 512].rearrange("p (s n) -> p s n", s=2)
                        z = (ZA if zi == 0 else ZG)[par]
                        zk = ("Z2", zi, par)
                        ACT(z, PS2[:, :, 1:257], AF.Identity, bkeys + ["vecs"], [zk], bias=vcol(VL + VL_FCB + ch), scale=vcol(VL + VL_FCW + 44 + ch))
                        STT(z, PS2[:, :, 0:256], vcol(VL + VL_FCW + ch), z, ALU.mult, ALU.add, bkeys + [zk, "vecs"], [zk])
                        STT(z, PS2[:, :, 2:258], vcol(VL + VL_FCW + 88 + ch), z, ALU.mult, ALU.add, bkeys + [zk, "vecs"], [zk])
                        res.append((z, zk))
                    sgb = TMPB[par][:, 0:512].rearrange("p (s n) -> p s n", s=2)
                    sgk = ("TMPB", par)
                    ACT(sgb, res[1][0], AF.Silu, [res[1][1]], [sgk])
                    TT(AV[:, pi, s2 * 512:(s2 + 1) * 512].rearrange("p (s n) -> p s n", s=2), sgb, res[0][0], ALU.mult,
                       [sgk, res[0][1]], [("AV", pi, s2)], eng="pool")
            for oc in range(8):
                w1, w1k = wget("w_down", l, 0, 11, oc * 128, 128)
                w2, w2k = wget("w_down", l, 11 * 128, 11, oc * 128, 128)
                for t2 in range(2):
                    t = hf * 2 + t2
                    bk, bkk = bank()
                    items, rd = [], [w1k, w2k]
                    for k in range(22):
                        wv = w1 if k < 11 else w2
                        items.append((bk[:], wv[:, k % 11, :], AV[:, k, t2 * 512:(t2 + 1) * 512], k == 0, k == 21))
                        rd.append(("AV", k, t2))
                    MM(items, rd, [bkk])
                    STT(X[:, oc, tsl(t)], bk[:], modv[:, l * 48 + 40 + oc:l * 48 + 41 + oc], X[:, oc, tsl(t)], ALU.mult, ALU.add,
                        [bkk, ("modv", l, 1), xkeys(oc, t)], [xkeys(oc, t)])
        if dbg and l == 1:
            DMA(dbgX[2, :, :], X[:].rearrange("p c t -> p (c t)"), [xkeys(c, t) for c in range(8) for t in range(NT)], [], "dbgX", is_out=True)

    try:
        for l in range(2):
            layer(l)
            chk(l, 8)
    except Stop:
        if dbg:
            P.barrier()
            DMA(dbgY[1, :, :], Y[:].rearrange("p c t -> p (c t)"), [], [], "dbgY", is_out=True)
            P.barrier()
            DMA(dbgX[2, :, :], X[:].rearrange("p c t -> p (c t)"), [], [], "dbgX", is_out=True)

    P.barrier()
    YT = AR[:, 0:8 * 512 * 2].bitcast(F32).rearrange("p (c n) -> p c n", c=8)
    OST = [AR[:, 8192 + j * 2048:8192 + (j + 1) * 2048].bitcast(F32) for j in range(4)]
    for t in range(NT):
        rs, rsk = stats(t, final=True)
        for c in range(8):
            STT(YT[:, c, :], X[:, c, tsl(t)], vcol(V_FNG + c), rs, ALU.mult, ALU.mult, [xkeys(c, t), "vecs", rsk], [("YT", c)])
        for i4 in range(4):
            i = t * 4 + i4
            ost, ostk = OST[i % 4], ("ost", i % 4)
            for hf in range(2):
                bk, bkk = bank()
                TR([(bk[:, j * 128:(j + 1) * 128], YT[:, hf * 4 + j, i4 * 128:(i4 + 1) * 128], ident_f) for j in range(4)],
                   [("YT", hf * 4 + j) for j in range(4)] + ["CF"], [bkk])
                if hf == 0:
                    ACT(ost[:, 0:512], bk[:], AF.Copy, [bkk], [ostk])
                else:
                    P.op("dve", lambda e, bk=bk, ost=ost: e.tensor_copy(out=ost[:, 512:1024], in_=bk[:]), [bkk], [ostk])
            DMA(y_out[i * 128:(i + 1) * 128, :], ost, [ostk], [], "ost%d" % (i % 4), is_out=True)
    return P, rec


V_CVEC = 0
V_CARRY = 8
V_CTXB = 9
V_HLB = 10
V_FNG = 18
VL_N1G, VL_N2G, VL_BADA, VL_CAW, VL_CAB, VL_HNG, VL_BGATE, VL_FCW, VL_FCB = 0, 8, 16, 64, 70, 72, 74, 106, 238
VL_SIZE = 282
V_L = [26, 26 + VL_SIZE]
NV = 26 + 2 * VL_SIZE


def colv(v):
    v = np.asarray(v, np.float32).reshape(-1, 128)
    return np.ascontiguousarray(v.T)


_CACHE = {}


def get_nc(dbg=False, upto=99):
    key = ("nc", dbg, upto)
    if key not in _CACHE:
        _, rec = build(None, dbg, upto)
        P, _ = build(rec, dbg, upto)
        _CACHE[key] = P.finish()
    return _CACHE[key]


def host_prep(inputs, core):
    f = lambda a: np.ascontiguousarray(np.asarray(a, np.float32))
    prompt = core < 4
    m = {}
    if prompt:
        m["x_in"] = f(inputs["x_prompt"][core * 8:(core + 1) * 8].reshape(T, D))
        cvec = inputs["c_ctx"]
    else:
        b = core - 4
        m["x_in"] = f(inputs["x_sample"][b])
        cvec = inputs["c"][b]
    vecs = np.zeros((128, NV), np.float32)
    vecs[:, V_CVEC:V_CVEC + 8] = colv(cvec)
    vecs[:, V_CARRY] = 0.0 if prompt else 1.0
    vecs[:, V_CTXB] = NEG if prompt else 0.0
    hl = np.asarray(inputs["hgrn_lb"], np.float32)
    vecs[:, V_HLB:V_HLB + 4] = colv(hl[0])
    vecs[:, V_HLB + 4:V_HLB + 8] = colv(hl[1])
    vecs[:, V_FNG:V_FNG + 8] = colv(inputs["final_norm_g"])
    for l in range(2):
        o = V_L[l]
        vecs[:, o + VL_N1G:o + VL_N1G + 8] = colv(inputs["norm1_g"][l])
        vecs[:, o + VL_N2G:o + VL_N2G + 8] = colv(inputs["norm2_g"][l])
        vecs[:, o + VL_BADA:o + VL_BADA + 48] = colv(inputs["b_ada"][l])
        vecs[:, o + VL_CAW:o + VL_CAW + 6] = colv(inputs["conv_a_w"][l])
        vecs[:, o + VL_CAB:o + VL_CAB + 2] = colv(inputs["conv_a_b"][l])
        vecs[:, o + VL_HNG:o + VL_HNG + 2] = colv(inputs["hgrn_norm_g"][l])
        vecs[:, o + VL_BGATE:o + VL_BGATE + 32] = colv(inputs["b_gate"][l])
        vecs[:, o + VL_FCW:o + VL_FCW + 132] = colv(inputs["ffn_conv_w"][l])
        vecs[:, o + VL_FCB:o + VL_FCB + 44] = colv(inputs["ffn_conv_b"][l])
    m["vecs"] = vecs
    ii = np.arange(128)
    ident = np.eye(128, dtype=np.float32)
    blockmask = ((ii[:, None] // 64) == (ii[None, :] // 64)).astype(np.float32)
    scanmask = np.broadcast_to(((ii % 32) != 0).astype(np.float32)[None, :], (128, 128))
    cmask = (ii[:, None] // 32 == np.arange(4)[None, :]).astype(np.float32)
    m["c_f32"] = f(np.concatenate([ident, blockmask, scanmask, cmask], axis=1))
    same = (ii[:, None] // 32) == (ii[None, :] // 32)
    trif = (same & (ii[:, None] <= ii[None, :])).astype(np.float32)
    trib = (same & (ii[:, None] >= ii[None, :])).astype(np.float32)
    m["c_bf"] = f(np.concatenate([ident, np.ones((128, 128), np.float32), blockmask, trif, trib], axis=1))
    m["gng_rep"] = f(np.broadcast_to(np.asarray(inputs["gmlp_norm_g"], np.float32)[:, None, :], (2, 128, 256)))
    m["gmlp_brow"] = f(np.asarray(inputs["gmlp_b"], np.float32).reshape(2, 1, 512))
    ws = np.asarray(inputs["gmlp_ws"], np.float32)
    m["gmlp_wsT"] = f(ws.transpose(0, 3, 1, 2).reshape(2, 128, 512))
    s0 = np.zeros((2, 2, 2, 128, 128), np.float32)
    ctxk = np.zeros((2, 512, 256), np.float32)
    ctxv = np.zeros((2, 512, 256), np.float32)
    if not prompt:
        st = np.asarray(inputs["state_hgrn"], np.float32)[core - 4]
        for hp in range(2):
            for hh in range(2):
                s0[:, :, hp, hh * 64:(hh + 1) * 64, hh * 64:(hh + 1) * 64] = st[:, :, 2 * hp + hh]
        ck = np.asarray(inputs["cache_k"], np.float32)[core - 4]
        cv = np.asarray(inputs["cache_v"], np.float32)[core - 4]
        ctxk = f(ck.transpose(0, 2, 1, 3).reshape(2, 512, 256))
        ctxv = f(cv.transpose(0, 2, 1, 3).reshape(2, 512, 256))
    m["s0"] = s0
    m["ctxk"] = ctxk
    m["ctxv"] = ctxv
    rpb = np.asarray(inputs["na_rpb"], np.float32)
    nab = np.full((2, 2, 128, NCLS, 2, 5, 128), NEG, np.float32)
    rep = {0: 0, 1: 1, 2: 2, 3: 3, 4: 14, 5: 15}
    kk = np.arange(128)
    krow_l, kcol = kk // 64, kk % 64
    qrow_l, qcol = kk // 64, kk % 64
    for cls, mrep in rep.items():
        t0 = tile0_of(mrep)
        for j in range(5):
            krow = 2 * (t0 + j) + krow_l
            qrow = 2 * mrep + qrow_l
            if prompt:
                ok = (krow[:, None] // 4) == (qrow[None, :] // 4)
                val = np.where(ok, 0.0, NEG).astype(np.float32)
                nab[:, :, :, cls, :, j, :] = val[None, None, :, None, :]
            else:
                kr0 = np.clip(qrow - 4, 0, 24)
                rowok = (krow[:, None] >= kr0[None, :]) & (krow[:, None] < kr0[None, :] + 8)
                ws_ = np.clip(qcol - 8, 0, 48)
                colok = (kcol[:, None] >= ws_[None, :]) & (kcol[:, None] < ws_[None, :] + 16)
                ok = rowok & colok
                drow = np.clip(krow[:, None] - qrow[None, :] + 7, 0, 14)
                dcol = np.clip(kcol[:, None] - qcol[None, :] + 15, 0, 30)
                for l in range(2):
                    for hp in range(2):
                        for hh in range(2):
                            g = rpb[l, 2 * hp + hh][drow, dcol]
                            nab[l, hp, :, cls, hh, j, :] = np.where(ok, g, NEG)
    m["nabias"] = f(nab.reshape(2, 2, 128, NCLS * 2 * 5 * 128))
    return m


WNAMES = ("w_ada", "w_in", "w_gate", "w_br_a", "w_br_b", "w_br_c", "w_br_d", "w_o", "w_up", "w_down")


def kernel(**inputs):
    dbg = bool(inputs.pop("_dbg", False))
    upto = int(inputs.pop("_upto", 99))
    nc = get_nc(dbg, upto)
    wts = {nm: np.ascontiguousarray(np.asarray(inputs[nm], np.float32)) for nm in WNAMES}
    in_maps = []
    for core in range(8):
        m = host_prep(inputs, core)
        m.update(wts)
        in_maps.append(m)
    res = run_bass_kernel_spmd(nc, in_maps, core_ids=list(range(8)))
    R = res.results
    y_prompt = np.concatenate([R[c]["y_out"].reshape(8, 256, D) for c in range(4)], axis=0)
    y_sample = np.stack([R[c]["y_out"] for c in range(4, 8)], axis=0)
    def cache(name):
        outs = []
        for c in range(4):
            a = R[c][name].reshape(2, 8, 256, 4, 64)
            outs.append(a.transpose(1, 0, 3, 2, 4))
        return np.ascontiguousarray(np.concatenate(outs, axis=0))
    nk, nv = cache("kc_out"), cache("vc_out")
    sts = []
    for c in range(4):
        a = R[c]["st_out"]
        o = np.zeros((8, 2, 2, 4, 64, 64), np.float32)
        for hp in range(2):
            for hh in range(2):
                blk = a[:, :, :, hp, hh * 64:(hh + 1) * 64, hh * 64:(hh + 1) * 64]
                o[:, :, :, 2 * hp + hh] = blk.transpose(1, 0, 2, 3, 4)
        sts.append(o)
    ns = np.concatenate(sts, axis=0)
    outs = (np.ascontiguousarray(y_prompt, dtype=np.float32), np.ascontiguousarray(y_sample, dtype=np.float32),
            nk.astype(np.float32), nv.astype(np.float32), ns.astype(np.float32))
    if dbg:
        return outs, R
    return outs
```

```python
from contextlib import ExitStack
import numpy as np
import concourse.bass as bass
import concourse.mybir as mybir
from concourse.bass_utils import run_bass_kernel_spmd

F32 = mybir.dt.float32
BF16 = mybir.dt.bfloat16
AF = mybir.ActivationFunctionType
ALU = mybir.AluOpType
ENGS = ("pe", "act", "dve", "pool", "sp")
D = 1024
T = 2048
NSEG = T // 256
NT = T // 512
NT128 = T // 128
DFF = 2816
EPS = 1e-6
NEG = -30000.0
O_AX, O_AB, O_AC, O_BQ, O_BFF, O_BFB, O_BI, O_BG, O_CQ, O_CK, O_CV, O_DU, O_DV = (
    0, 256, 512, 768, 1024, 1280, 1536, 1792, 2048, 2304, 2560, 2816, 3072)
NCLS = 6


def cls_of(m):
    return {0: 0, 1: 1, 14: 4, 15: 5}.get(m, 2 + (m % 2))


def tile0_of(m):
    return min(max(m - 2, 0), NT128 - 5)


class Prog:
    def __init__(self):
        self.nc = bass.Bass("TRN2", target_bir_lowering=False)
        self.stack = ExitStack()
        self.streams = {e: [] for e in ENGS}
        self.cnt = {}
        self.lastw = {}
        self.readers = {}
        self.known = {e: {} for e in ENGS}
        self.out_sigs = []
        self.bank_i = 0
        self.pending = {}
        self.capture = None

    def barrier(self):
        snap = dict(self.cnt)
        for e in ENGS:
            p = self.pending.setdefault(e, {})
            for s_, v in snap.items():
                if self.known[e].get(s_, 0) < v and p.get(s_, 0) < v:
                    p[s_] = v

    def sb(self, name, shape, dt):
        return self.stack.enter_context(self.nc.sbuf_tensor(name, list(shape), dt))

    def ps(self, name, shape, dt=F32):
        return self.stack.enter_context(self.nc.psum_tensor(name, list(shape), dt))

    def dram(self, name, shape, dt, kind):
        return self.nc.dram_tensor(name, list(shape), dt, kind=kind).ap()

    def op(self, eng, emit, reads=(), writes=(), sem=None, out=False):
        if self.capture is not None:
            self.capture.append((eng, emit, tuple(reads), tuple(writes), sem, out))
            return None
        if sem is None:
            s = "E_" + eng
            inc = 1
        else:
            s = "D_" + sem
            inc = 16
        deps = []
        for k in reads:
            w = self.lastw.get(k)
            if w is not None:
                deps.append(w)
        for k in writes:
            w = self.lastw.get(k)
            if w is not None:
                deps.append(w)
            deps.extend(self.readers.get(k, ()))
        kn = self.known[eng]
        need = dict(self.pending.pop(eng, {}))
        for (ds, dv) in deps:
            if kn.get(ds, 0) >= dv:
                continue
            if eng == "pe" and ds == "E_pe":
                continue
            if need.get(ds, 0) < dv:
                need[ds] = dv
        for ds, dv in need.items():
            kn[ds] = dv
        self.cnt[s] = self.cnt.get(s, 0) + inc
        sig = (s, self.cnt[s])
        self.streams[eng].append((tuple(need.items()), emit, (s, inc)))
        for k in reads:
            self.readers.setdefault(k, []).append(sig)
        for k in writes:
            self.lastw[k] = sig
            self.readers[k] = []
        if out:
            self.out_sigs.append(sig)
        return sig

    def finish(self):
        nc = self.nc
        fin = {}
        for (s, v) in self.out_sigs:
            fin[s] = max(fin.get(s, 0), v)
        sems = {}
        for s in self.cnt:
            sems[s] = self.stack.enter_context(nc.semaphore(s))
        streams = self.streams

        def run(engname, eng):
            for (waits, emit, (s, inc)) in streams[engname]:
                for (ws, wv) in waits:
                    eng.wait_ge(sems[ws], wv)
                inst = emit(eng)
                inst.then_inc(sems[s], inc)
            if engname == "sp":
                for s, v in fin.items():
                    eng.wait_ge(sems[s], v)

        with nc.Block() as block:
            @block.tensor
            def _(e):
                run("pe", e)

            @block.scalar
            def _(e):
                run("act", e)

            @block.vector
            def _(e):
                run("dve", e)

            @block.gpsimd
            def _(e):
                run("pool", e)

            @block.sync
            def _(e):
                run("sp", e)
        self.stack.close()
        return nc


import os
HGL = int(os.environ.get("HGL", "9"))
WSLOT = 2048
NWS = 4


class Stop(Exception):
    pass


def build(plan=None, dbg=False, upto=99):
    P = Prog()
    nc = P.nc
    rec = []
    x_in = P.dram("x_in", [T, D], F32, "ExternalInput")
    vecs_d = P.dram("vecs", [128, NV], F32, "ExternalInput")
    cf_d = P.dram("c_f32", [128, 3 * 128 + 4], F32, "ExternalInput")
    cb_d = P.dram("c_bf", [128, 5 * 128], F32, "ExternalInput")
    gng_d = P.dram("gng_rep", [2, 128, 256], F32, "ExternalInput")
    gb_d = P.dram("gmlp_brow", [2, 1, 512], F32, "ExternalInput")
    wsT_d = P.dram("gmlp_wsT", [2, 128, 512], F32, "ExternalInput")
    s0_d = P.dram("s0", [2, 2, 2, 128, 128], F32, "ExternalInput")
    nab_d = P.dram("nabias", [2, 2, 128, NCLS * 2 * 5 * 128], F32, "ExternalInput")
    ctxk_d = P.dram("ctxk", [2, 512, 256], F32, "ExternalInput")
    ctxv_d = P.dram("ctxv", [2, 512, 256], F32, "ExternalInput")
    WD = {}
    for nm, shp in (("w_ada", [2, D, 6 * D]), ("w_in", [2, D, 3328]), ("w_gate", [2, D, 4 * D]),
                    ("w_br_a", [2, 256, D]), ("w_br_b", [2, 256, D]), ("w_br_c", [2, 256, D]),
                    ("w_br_d", [2, 256, D]), ("w_o", [2, D, D]), ("w_up", [2, D, 2 * DFF]),
                    ("w_down", [2, DFF, D])):
        WD[nm] = P.dram(nm, shp, F32, "ExternalInput")
    y_out = P.dram("y_out", [T, D], F32, "ExternalOutput")
    kc_out = P.dram("kc_out", [2, T, 256], F32, "ExternalOutput")
    vc_out = P.dram("vc_out", [2, T, 256], F32, "ExternalOutput")
    st_out = P.dram("st_out", [2, NSEG, 2, 2, 128, 128], F32, "ExternalOutput")
    if dbg:
        dbgY = P.dram("dbgY", [2, 128, 8 * T], BF16, "ExternalOutput")
        dbgX = P.dram("dbgX", [3, 128, 8 * T], F32, "ExternalOutput")

    X = P.sb("X", [128, 8, T], F32)
    H = P.sb("H", [128, 8, NSEG, 258], BF16)
    Y = P.sb("Y", [128, 8, T], BF16)
    AR = P.sb("AR", [128, 12 * T], BF16)
    WS = [P.sb("WS%d" % i, [128, WSLOT], BF16) for i in range(NWS)]
    STG = [P.sb("STG%d" % i, [128, 256], F32) for i in range(2)]
    vecs = P.sb("vecs_sb", [128, NV], F32)
    CF = P.sb("CF", [128, 3 * 128 + 4], F32)
    CB = P.sb("CBc", [128, 5 * 128], BF16)
    GNG = AR[:, 4 * T:4 * T + 512].bitcast(F32)
    WST = AR[:, 4 * T + 512:4 * T + 1024]
    GBR = AR[0:1, 4 * T + 1024:4 * T + 2048].bitcast(F32)
    ONEF = AR[0:1, 5 * T:5 * T + 256].bitcast(F32)
    modv = P.sb("modv", [128, 2 * 48], F32)
    lbv = P.sb("lbv", [128, 2, 2, 4], F32)
    ABv = P.sb("ABv", [128, 16], F32)
    SCb = P.sb("SCb", [128, 8], BF16)
    SQ = [P.sb("SQ0", [128, 512], BF16)] * 2
    RS = P.sb("RS", [128, 512], F32)
    RSTD = RS
    TMPF = [P.sb("TMPF0", [128, 512], F32)] * 2
    TMPB = [P.sb("TMPB%d" % i, [128, 512], BF16) for i in range(2)]
    SML = P.sb("SML", [128, 8], F32)
    SST = [P.sb("SST0", [128, 128], F32)] * 2
    SBF = [P.sb("SBF0", [128, 128], BF16)] * 2
    SSTG = [STG[1][:, 0:128]] * 2
    PSALL = P.ps("psall", [128, 8 * 512], F32)
    BANKS = [PSALL[:, i * 512:(i + 1) * 512] for i in range(8)]

    ident_f = CF[:, 0:128]
    blockmask = CF[:, 128:256]
    scanmask = CF[:, 256:384]
    cmask = CF[:, 384:388]
    ident_b = CB[:, 0:128]
    ones_b = CB[:, 128:256]
    blockones = CB[:, 256:384]
    trimask = [CB[:, 384:512], CB[:, 512:640]]

    def bank():
        i = P.bank_i % 8
        P.bank_i += 1
        return BANKS[i], ("bank", i)

    def plane(i, n=1, dt=BF16):
        a = AR[:, i * T:(i + n) * T]
        return a.bitcast(F32) if dt == F32 else a

    def vcol(c, n=1):
        return vecs[:, c:c + n]

    def ACT(out, in_, func, reads, writes, bias=None, scale=None, accum=None):
        def emit(e):
            kw = {}
            if bias is not None:
                kw["bias"] = bias
            if scale is not None:
                kw["scale"] = scale
            if accum is not None:
                kw["accum_out"] = accum
            return e.activation(out=out, in_=in_, func=func, **kw)
        P.op("act", emit, reads, writes)

    def TT(out, a, b, op, reads, writes, eng="dve"):
        P.op(eng, lambda e: e.tensor_tensor(out=out, in0=a, in1=b, op=op), reads, writes)

    def STT(out, in0, scalar, in1, op0, op1, reads, writes):
        P.op("dve", lambda e: e.scalar_tensor_tensor(out=out, in0=in0, scalar=scalar, in1=in1, op0=op0, op1=op1),
             reads, writes)

    def TS(out, in0, s1, s2, op0, op1, reads, writes, eng="dve"):
        if s2 is None:
            P.op(eng, lambda e: e.tensor_scalar(out=out, in0=in0, scalar1=s1, scalar2=None, op0=op0), reads, writes)
        else:
            P.op(eng, lambda e: e.tensor_scalar(out=out, in0=in0, scalar1=s1, scalar2=s2, op0=op0, op1=op1),
                 reads, writes)

    def MEMSET(ap, val, writes, eng="dve"):
        P.op(eng, lambda e: e.memset(ap, val), (), writes)

    def DMA(out, in_, reads, writes, sem, eng="sp", is_out=False):
        P.op(eng, lambda e: e.dma_start(out=out, in_=in_), reads, writes, sem=sem, out=is_out)

    def MM(items, reads, writes):
        def emit(e):
            inst = None
            for it in items:
                kw = {}
                if len(it) > 5 and it[5] is not None:
                    kw["tile_position"] = it[5]
                inst = e.matmul(it[0], lhsT=it[1], rhs=it[2], start=it[3], stop=it[4], **kw)
            return inst
        P.op("pe", emit, reads, writes)

    def TR(items, reads, writes):
        def emit(e):
            inst = None
            for (o, i, idn) in items:
                inst = e.transpose(out=o, in_=i, identity=idn)
            return inst
        P.op("pe", emit, reads, writes)

    wstate = {"i": 0, "issued": 0}

    def w_issue(desc, idx):
        (nm, l, r0, kc, c0, ncols) = desc
        slot = idx % NWS
        src = WD[nm][l, r0:r0 + kc * 128, c0:c0 + ncols].rearrange("(k p) n -> p k n", p=128)
        dst = WS[slot][:, 0:kc * ncols].rearrange("p (k n) -> p k n", k=kc)
        DMA(dst, src, (), [("w", slot)], sem="w%d" % slot, eng="pool")

    def wget(nm, l, r0, kc, c0, ncols):
        desc = (nm, l, r0, kc, c0, ncols)
        assert kc * ncols <= WSLOT
        rec.append(desc)
        i = wstate["i"]
        if plan is None:
            w_issue(desc, i)
        else:
            assert plan[i] == desc, (plan[i], desc)
            while wstate["issued"] < min(len(plan), i + NWS - 1):
                w_issue(plan[wstate["issued"]], wstate["issued"])
                wstate["issued"] += 1
        wstate["i"] = i + 1
        slot = i % NWS
        return WS[slot][:, 0:kc * ncols].rearrange("p (k n) -> p k n", k=kc), ("w", slot)

    DMA(vecs[:], vecs_d[:, :], (), ["vecs"], "vecs")
    DMA(CF[:], cf_d[:, :], (), ["CF"], "CF")
    DMA(CB[:], cb_d[:, :], (), ["CB"], "CB", eng="pool")
    CONST = ["vecs", "CF", "CB"]

    MEMSET(lbv[:, 0, 0, :], 0.0, [("lbv", 0)])
    MEMSET(lbv[:, 0, 1, :], 1.0, [("lbv", 0)])
    TT(lbv[:, 1, 0, :], vcol(V_HLB + 4, 4), vcol(V_HLB, 4), ALU.subtract, ["vecs"], [("lbv", 1)])
    ACT(lbv[:, 1, 0, :], lbv[:, 1, 0, :], AF.Sigmoid, [("lbv", 1)], [("lbv", 1)])
    TS(lbv[:, 1, 1, :], lbv[:, 1, 0, :], -1.0, 1.0, ALU.mult, ALU.add, [("lbv", 1)], [("lbv", 1)])

    XST = [AR[:, j * 2048:(j + 1) * 2048].bitcast(F32) for j in range(8)]
    for i in range(NT128):
        st, stk = XST[i % 8], ("xst", i % 8)
        DMA(st, x_in[i * 128:(i + 1) * 128, :], (), [stk], "xst%d" % (i % 8))
        for hf in range(2):
            bk, bkk = bank()
            TR([(bk[:, j * 128:(j + 1) * 128], st[:, (hf * 4 + j) * 128:(hf * 4 + j + 1) * 128], ident_f) for j in range(4)],
               [stk, "CF"], [bkk])
            wr = [("X", c, i // 4) for c in range(hf * 4, hf * 4 + 4)]
            if hf == 0:
                ACT(X[:, 0:4, i * 128:(i + 1) * 128], bk[:].rearrange("p (j t) -> p j t", j=4), AF.Copy, [bkk], wr)
            else:
                P.op("dve", lambda e, bk=bk, i=i: e.tensor_copy(out=X[:, 4:8, i * 128:(i + 1) * 128], in_=bk[:].rearrange("p (j t) -> p j t", j=4)), [bkk], wr)

    ACT(SCb[:], vcol(V_CVEC, 8), AF.Silu, ["vecs"], ["SCb"])

    def mod_group(l, g, fixed_bank=None):
        wv, wk = wget("w_ada", l, 0, 8, g * 256, 256)
        bk, bkk = bank() if fixed_bank is None else (BANKS[fixed_bank], ("bank", fixed_bank))
        items = []
        for jj in range(2):
            for k in range(8):
                items.append((bk[:, jj:jj + 1], wv[:, k, jj * 128:(jj + 1) * 128], SCb[:, k:k + 1], k == 0, k == 7))
        MM(items, [wk, "SCb"], [bkk])
        TT(modv[:, l * 48 + g * 2:l * 48 + g * 2 + 2], bk[:, 0:2], vcol(V_L[l] + VL_BADA + g * 2, 2), ALU.add,
           [bkk, "vecs"], [("modv", l, int(g >= 8))])
    for g in range(8):
        mod_group(0, g)
    deferred = [(0, g) for g in range(8, 24)] + [(1, g) for g in range(24)]

    def run_deferred(n=1, fixed_bank=None):
        for _ in range(n):
            if deferred:
                mod_group(*deferred.pop(0), fixed_bank=fixed_bank)

    def xkeys(c, t):
        return ("X", c, t)

    def stats(t, final=False):
        par = t % 2
        xk = [xkeys(c, t) for c in range(8)]
        if final:
            sq = AR[:, 16384 + par * 4096:16384 + (par + 1) * 4096].rearrange("p (c n) -> p c n", c=8)
            rs = [RS[:], TMPF[0][:]][par]
        else:
            sq = AR[:, par * 4096:(par + 1) * 4096].rearrange("p (c n) -> p c n", c=8)
            rs = AR[:, 8192 + par * 1024:8192 + (par + 1) * 1024].bitcast(F32)
        sqk, rsk = ("NSQ", par), ("NRS", par)
        ACT(sq, X[:, :, t * 512:(t + 1) * 512], AF.Square, xk, [sqk])
        bk, bkk = bank()
        MM([(bk[:], ones_b, sq[:, c, :], c == 0, c == 7) for c in range(8)], [sqk, "CB"], [bkk])
        ACT(rs, bk[:], AF.Sqrt, [bkk, "epsv"], [rsk], bias=SML[:, 0:1], scale=1.0 / D)
        P.op("dve", lambda e: e.reciprocal(out=rs, in_=rs), [rsk], [rsk])
        return rs, rsk

    MEMSET(SML[:, 0:1], EPS, ["epsv"])

    def norm_mod(l, which):
        gcol = V_L[l] + (VL_N1G if which == 0 else VL_N2G)
        sh = modv[:, l * 48 + which * 24:l * 48 + which * 24 + 8]
        sc = modv[:, l * 48 + which * 24 + 8:l * 48 + which * 24 + 16]
        STT(ABv[:, 0:8], sc, 1.0, vcol(gcol, 8), ALU.add, ALU.mult, [("modv", l, which), "vecs"], ["ABv"])
        TF = AR[:, 10240:10240 + 8192].bitcast(F32).rearrange("p (c n) -> p c n", c=8)
        for t in range(NT):
            rs, rsk = stats(t)
            TT(TF, X[:, :, t * 512:(t + 1) * 512], rs.unsqueeze(1).to_broadcast([128, 8, 512]), ALU.mult,
               [xkeys(c, t) for c in range(8)] + [rsk], ["NTF"])
            for c in range(8):
                ACT(H[:, c, 2 * t:2 * t + 2, 1:257], TF[:, c, :].rearrange("p (s n) -> p s n", s=2), AF.Identity,
                    ["NTF", "ABv", ("modv", l, which)], [("H", c, t)], bias=sh[:, c:c + 1], scale=ABv[:, c:c + 1])

    def h_rhs(k, t):
        return H[:, k, 2 * t:2 * t + 2, 1:257], [("H", k, t)]

    def lin_fm(nm, l, col0, nchunks, kc, rhs_fn, ntiles, evac, r0=0):
        oc = 0
        while oc < nchunks:
            ng = min(2, nchunks - oc)
            wv, wk = wget(nm, l, r0, kc, col0 + oc * 128, ng * 128)
            for jj in range(ng):
                for t in range(ntiles):
                    bk, bkk = bank()
                    items, rd = [], [wk]
                    for k in range(kc):
                        r, rk = rhs_fn(k, t)
                        items.append((bk[:], wv[:, k, jj * 128:(jj + 1) * 128], r, k == 0, k == kc - 1))
                        rd += rk
                    MM(items, rd, [bkk])
                    evac(oc + jj, t, bk, bkk)
            oc += ng

    def lin_tm(l, col0, ncols, evac):
        wv, wk = wget("w_in", l, 0, 8, col0, ncols)
        for i in range(NT128):
            seg, off = (i * 128) // 256, (i * 128) % 256
            bk, bkk = bank()
            items = [(bk[:, 0:ncols], H[:, k, seg, 1 + off:1 + off + 128], wv[:, k, :], k == 0, k == 7) for k in range(8)]
            MM(items, [wk] + [("H", k, i // 4) for k in range(8)], [bkk])
            evac(i, bk, bkk)

    def tsl(t):
        return slice(t * 512, (t + 1) * 512)

    def chk(l, stage):
        if l == 0 and upto <= stage:
            raise Stop()

    def layer(l):
        VL = V_L[l]
        P.barrier()
        chk(l, 1)
        norm_mod(l, 0)

        P.barrier()
        AXp, ABp, PXp, Zp = 0, 2, 4, 7
        PX = AR[:, PXp * T:PXp * T + 2 * NSEG * 258].rearrange("p (c s n) -> p c s n", c=2, s=NSEG)

        def evacA(oc, t, bk, bkk):
            c = oc % 2
            if oc < 2:
                ACT(plane(AXp + c)[:, tsl(t)], bk[:], AF.Copy, [bkk], [("AX", c, t)])
            elif oc < 4:
                ACT(plane(ABp + c)[:, tsl(t)], bk[:], AF.Copy, [bkk], [("ABp", c, t)])
            else:
                TT(PX[:, c, 2 * t:2 * t + 2, 1:257], bk[:].rearrange("p (s n) -> p s n", s=2),
                   plane(AXp + c)[:, tsl(t)].rearrange("p (s n) -> p s n", s=2), ALU.mult,
                   [bkk, ("AX", c, t)], [("PX", c, t)])
        lin_fm("w_in", l, O_AX, 6, 8, h_rhs, NT, evacA)
        for c in range(2):
            pk = [("PX", c, t) for t in range(NT)]
            MEMSET(PX[:, c, 0, 0:1], 0.0, [("PXh", c)])
            MEMSET(PX[:, c, NSEG - 1, 257:258], 0.0, [("PXh", c)])
            TS(PX[:, c, 1:NSEG, 0:1], PX[:, c, 0:NSEG - 1, 256:257], vcol(V_CARRY), None, ALU.mult, None,
               pk + ["vecs"], [("PXh", c)])
            TS(PX[:, c, 0:NSEG - 1, 257:258], PX[:, c, 1:NSEG, 1:2], vcol(V_CARRY), None, ALU.mult, None,
               pk + ["vecs"], [("PXh", c)])
            Z = plane(Zp + 2 * c, 2, F32).rearrange("p (s n) -> p s n", s=NSEG)
            zk = ("Z", c)
            allp = pk + [("PXh", c), "vecs"]
            TS(Z, PX[:, c, :, 1:257], vcol(VL + VL_CAW + 2 + c), vcol(VL + VL_CAB + c), ALU.mult, ALU.add, allp, [zk])
            STT(Z, PX[:, c, :, 0:256], vcol(VL + VL_CAW + 0 + c), Z, ALU.mult, ALU.add, allp + [zk], [zk])
            STT(Z, PX[:, c, :, 2:258], vcol(VL + VL_CAW + 4 + c), Z, ALU.mult, ALU.add, allp + [zk], [zk])
            TT(Y[:, 0 + c, :], plane(Zp + 2 * c, 2, F32), plane(ABp + c), ALU.mult,
               [zk] + [("ABp", c, t) for t in range(NT)], [("Y", 0 + c)])

        chk(l, 2)
        P.barrier()
        Up, VGp = 0, 2
        VG = AR[:, VGp * T:(VGp + 2) * T].rearrange("p (i n) -> p i n", i=NT128)
        DMA(GNG, gng_d[l, :, :], (), ["GNG"], "GNG")
        DMA(GBR, gb_d[l, :, :], (), ["GBR"], "GBR")
        DMA(WST, wsT_d[l, :, :], (), ["WST"], "WST", eng="pool")
        MEMSET(ONEF, 1.0, ["ONEF"])

        def evacDu(oc, t, bk, bkk):
            ACT(plane(Up + oc)[:, tsl(t)], bk[:], AF.Copy, [bkk], [("U", oc, t)])
        lin_fm("w_in", l, O_DU, 2, 8, h_rhs, NT, evacDu)

        def evacDv(i, bk, bkk):
            pr = i % 2
            c0 = 1 + 3 * pr
            ACT(TMPB[pr][:, 0:256], bk[:, 0:256], AF.Square, [bkk], [("TMPB", pr), ("SMLd", pr, 0)], accum=SML[:, c0:c0 + 1])
            ACT(SML[:, c0 + 1:c0 + 2], SML[:, c0:c0 + 1], AF.Sqrt, [("SMLd", pr, 0), "epsv"], [("SMLd", pr, 1)], bias=SML[:, 0:1], scale=1.0 / 256)
            P.op("dve", lambda e: e.reciprocal(out=SML[:, c0 + 2:c0 + 3], in_=SML[:, c0 + 1:c0 + 2]), [("SMLd", pr, 1)], [("SMLd", pr, 2)])
            STT(VG[:, i, :], bk[:, 0:256], SML[:, c0 + 2:c0 + 3], GNG, ALU.mult, ALU.mult, [bkk, ("SMLd", pr, 2), "GNG"], [("VG", i)])
        lin_tm(l, O_DV, 256, evacDv)
        for i in range(NT128):
            for dc in range(2):
                bk, bkk = bank()
                items = []
                for gg in range(2):
                    g = 2 * dc + gg
                    o = bk[:, gg * 128:(gg + 1) * 128]
                    items.append((o, VG[:, i, dc * 128:(dc + 1) * 128], WST[:, g * 128:(g + 1) * 128], True, False))
                    items.append((o, ONEF[0:1, :], GBR[0:1, g * 128:(g + 1) * 128], False, True))
                MM(items, [("VG", i), "WST", "GBR", "ONEF"], [bkk])
                for gg in range(2):
                    rs_ = slice(gg * 64, (gg + 1) * 64)
                    TT(Y[rs_, 6 + dc, i * 128:(i + 1) * 128], bk[rs_, gg * 128:(gg + 1) * 128],
                       plane(Up + dc)[rs_, i * 128:(i + 1) * 128], ALU.mult, [bkk, ("U", dc, i // 4)], [("Y", 6 + dc, i, gg)])

        chk(l, 3)
        P.barrier()
        NBT = NCLS * 2 * 5 * 128
        oVT, oQ, oK, oBT = 0, 4352, 6400, 8448
        oKC = oBT + NBT
        oVC = oKC + 1024
        oPT = 18432
        assert oVC + 1040 <= oPT
        VTA = AR[:, oVT:oVT + NT128 * 260].rearrange("p (i h n) -> p i h n", i=NT128, h=4)
        Qf, Kf = AR[:, oQ:oQ + T], AR[:, oK:oK + T]
        EBT = AR[:, oBT:oBT + NBT].rearrange("p (c h j q) -> p c h j q", c=NCLS, h=2, j=5)
        KCT = AR[:, oKC:oKC + 1024].rearrange("p (h n) -> p h n", h=2)
        VCXA = AR[:, oVC:oVC + 1040].rearrange("p (j h n) -> p j h n", j=4, h=4)
        PT = [AR[:, oPT + i * 1152:oPT + (i + 1) * 1152] for i in range(2)]
        REC = AR[:, oPT + 2304:oPT + 2304 + 8].bitcast(F32)
        YTK = [AR[:, oPT + 2368 + i * 128:oPT + 2368 + (i + 1) * 128] for i in range(2)]
        assert oPT + 2368 + 256 <= 12 * T
        MEMSET(VTA[:, :, :, 64:65], 1.0, ["VTA1"])
        MEMSET(VCXA[:, :, :, 64:65], 1.0, ["VCX1"])

        def evacK(i, bk, bkk):
            s = i % 2
            ACT(STG[s][:, 0:256], bk[:, 0:256], AF.Copy, [bkk], [("stg", s)])
            DMA(kc_out[l, i * 128:(i + 1) * 128, :], STG[s][:, 0:256], [("stg", s)], [], "stg%d" % s, is_out=True)
        lin_tm(l, O_CK, 256, evacK)

        def evacV(i, bk, bkk):
            s = i % 2
            ACT(STG[s][:, 0:256], bk[:, 0:256], AF.Copy, [bkk], [("stg", s)])
            DMA(vc_out[l, i * 128:(i + 1) * 128, :], STG[s][:, 0:256], [("stg", s)], [], "stg%d" % s, is_out=True)
            P.op("dve", lambda e: e.tensor_copy(out=VTA[:, i, :, 0:64], in_=bk[:, 0:256].rearrange("p (h n) -> p h n", h=4)),
                 [bkk, "VTA1"], [("VT", i)])
        lin_tm(l, O_CV, 256, evacV)
        for jt in range(4):
            DMA(VCXA[:, jt, :, 0:64], ctxv_d[l, jt * 128:(jt + 1) * 128, :].rearrange("p (h n) -> p h n", h=4), ["VCX1"], ["VCX"], "VCX", eng="pool")
        for jt in range(4):
            s = jt % 2
            DMA(STG[s][:, 0:256], ctxk_d[l, jt * 128:(jt + 1) * 128, :], (), [("stg", s)], "stg%d" % s)
            bk, bkk = bank()
            TR([(bk[:, hp * 128:(hp + 1) * 128], STG[s][:, hp * 128:(hp + 1) * 128], ident_f) for hp in range(2)],
               [("stg", s), "CF"], [bkk])
            ACT(KCT[:, :, jt * 128:(jt + 1) * 128], bk[:, 0:256].rearrange("p (h n) -> p h n", h=2), AF.Copy,
                [bkk], [("KCT", jt)])
        for hp in range(2):
            DMA(AR[:, oBT:oBT + NBT], nab_d[l, hp, :, :], (), ["BT"], "BT", eng="pool")
            ACT(AR[:, oBT:oBT + NBT], AR[:, oBT:oBT + NBT], AF.Exp, ["BT"], ["BT"])

            def evacQ(oc, t, bk, bkk):
                ACT(Qf[:, tsl(t)], bk[:], AF.Copy, [bkk], [("Q", t)], scale=0.125)

            def evacKf(oc, t, bk, bkk):
                ACT(Kf[:, tsl(t)], bk[:], AF.Copy, [bkk], [("K", t)])
            lin_fm("w_in", l, O_CQ + hp * 128, 1, 8, h_rhs, NT, evacQ)
            lin_fm("w_in", l, O_CK + hp * 128, 1, 8, h_rhs, NT, evacKf)
            def st1(it, hp=hp):
                m, hh = it // 2, it % 2
                t0, cls = tile0_of(m), cls_of(m)
                rs_ = slice(hh * 64, (hh + 1) * 64)
                pt, ptk = PT[it % 2], ("PT", it % 2)
                qv = Qf[rs_, m * 128:(m + 1) * 128]
                bks = [(BANKS[3 * (it % 2) + j], ("bank", 3 * (it % 2) + j)) for j in range(3)]
                for b3 in range(3):
                    bk, bkk = bks[b3]
                    items, rd = [], [("Q", m // 4)]
                    for jj in range(4 if b3 < 2 else 1):
                        o = bk[:, jj * 128:(jj + 1) * 128]
                        if b3 == 0:
                            lhs = KCT[rs_, hp, jj * 128:(jj + 1) * 128]
                            rd.append(("KCT", jj))
                        else:
                            kt = t0 + (b3 - 1) * 4 + jj
                            lhs = Kf[rs_, kt * 128:(kt + 1) * 128]
                            rd.append(("K", kt // 4))
                        items.append((o, lhs, qv, True, True))
                    MM(items, rd, [bkk])
                ACT(pt[:, 0:512], bks[0][0][:], AF.Exp, [bks[0][1], "vecs"], [ptk], bias=vcol(V_CTXB))
                b1 = 3 * (it % 2) + 1
                ACT(pt[:, 512:1152], PSALL[:, b1 * 512:b1 * 512 + 640], AF.Exp, [bks[1][1], bks[2][1]], [ptk])
                TT(pt[:, 512:1152], pt[:, 512:1152], EBT[:, cls, hh, :, :].rearrange("p j q -> p (j q)"), ALU.mult,
                   [ptk, "BT"], [ptk])

            def st2(it, hp=hp):
                m, hh = it // 2, it % 2
                t0 = tile0_of(m)
                h = 2 * hp + hh
                pt, ptk = PT[it % 2], ("PT", it % 2)
                bo, bok = BANKS[6], ("bank", 6)
                items, rd = [], [ptk, "VCX", "VCX1", "VTA1"]
                for j in range(9):
                    if j < 4:
                        rhs = VCXA[:, j, h, :]
                    else:
                        rhs = VTA[:, t0 + j - 4, h, :]
                        rd.append(("VT", t0 + j - 4))
                    items.append((bo[:, hh * 65:hh * 65 + 65], pt[:, j * 128:(j + 1) * 128], rhs, j == 0, j == 8))
                MM(items, rd, [bok])
                if hh == 0:
                    return
                ytk = YTK[m % 2]
                ytkk = ("YTK", m % 2)
                P.op("dve", lambda e, bo=bo: e.reciprocal(out=REC[:, 0:2], in_=bo[:, 0:130].rearrange("p (h n) -> p h n", h=2)[:, :, 64]),
                     [bok], ["REC"])
                for h2 in range(2):
                    ACT(ytk[:, h2 * 64:(h2 + 1) * 64], bo[:, h2 * 65:h2 * 65 + 64], AF.Copy, [bok, "REC"], [ytkk], scale=REC[:, h2:h2 + 1])
                bt_, btk = BANKS[7], ("bank", 7)
                btb = bt_[:].bitcast(BF16)
                TR([(btb[:, 0:128], ytk, ident_b)], [ytkk, "CB"], [btk])
                P.op("dve", lambda e, btb=btb, m=m, hp=hp: e.tensor_copy(out=Y[:, 4 + hp, m * 128:(m + 1) * 128], in_=btb[:, 0:128]),
                     [btk], [("Y", 4 + hp, m, 0), ("Y", 4 + hp, m, 1)])

            NIT = 2 * NT128
            st1(0)
            for it in range(NIT):
                if it + 1 < NIT:
                    st1(it + 1)
                st2(it)

        chk(l, 4)
        P.barrier()
        LFp, LBp, KFp, KBp, QSp, SGp, VIp, OAp, TMp = 0, 2, 4, 5, 6, 7, 8, 9, 11
        tmb = TMp * T

        def tmpb(i):
            return AR[:, tmb + i * 128:tmb + (i + 1) * 128]

        def tmpf(i):
            return AR[:, tmb + 1024 + i * 256:tmb + 1024 + (i + 1) * 256].bitcast(F32)
        for hp in range(2):
            P.barrier()
            LG = [plane(LFp, 2, F32), plane(LBp, 2, F32)]
            KG = [plane(KFp), plane(KBp)]
            QS, SG, OA = plane(QSp), plane(SGp), plane(OAp, 2, F32)
            VI = AR[:, VIp * T:(VIp + 1) * T].rearrange("p (i n) -> p i n", i=NT128)

            def evacBq(oc, t, bk, bkk):
                ACT(QS[:, tsl(t)], bk[:], AF.Silu, [bkk], [("QS", t)])

            def mk_evacF(d):
                def ev(oc, t, bk, bkk):
                    lg = LG[d][:, tsl(t)]
                    ACT(lg, bk[:], AF.Sigmoid, [bkk], [("LG", d, t)])
                    TS(lg, lg, lbv[:, l, 1, d * 2 + hp:d * 2 + hp + 1], lbv[:, l, 0, d * 2 + hp:d * 2 + hp + 1], ALU.mult, ALU.add,
                       [("LG", d, t), ("lbv", l)], [("LG", d, t)])
                    TS(KG[d][:, tsl(t)], lg, -1.0, 1.0, ALU.mult, ALU.add, [("LG", d, t)], [("KG", d, t)])
                    ACT(lg, lg, AF.Ln, [("LG", d, t)], [("LG", d, t)])
                return ev

            def evacBg(oc, t, bk, bkk):
                ACT(SG[:, tsl(t)], bk[:], AF.Silu, [bkk], [("SG", t)])

            def evacBi(i, bk, bkk):
                P.op("dve", lambda e: e.tensor_copy(out=VI[:, i, :], in_=bk[:, 0:128]), [bkk], [("VI", i)])
            lin_fm("w_in", l, O_BQ + hp * 128, 1, 8, h_rhs, NT, evacBq)
            lin_fm("w_in", l, O_BFF + hp * 128, 1, 8, h_rhs, NT, mk_evacF(0))
            lin_fm("w_in", l, O_BFB + hp * 128, 1, 8, h_rhs, NT, mk_evacF(1))
            lin_fm("w_in", l, O_BG + hp * 128, 1, 8, h_rhs, NT, evacBg)
            lin_tm(l, O_BI + hp * 128, 128, evacBi)
            stg0b = STG[0][:].bitcast(BF16)
            TSETS = [
                dict(CUM=tmpf(0), E1=tmpf(1), EX=tmpf(2), D3=tmpf(3),
                     QT2=AR[:, tmb:tmb + 256], KT=tmpb(2), X32=AR[:, tmb + 384:tmb + 640], AM=AR[:, tmb + 640:tmb + 896],
                     KHT=tmpb(7), VIB=TMPB[0][:, 0:512]),
                dict(CUM=TMPF[0][:, 0:128], E1=TMPF[0][:, 128:256], EX=TMPF[0][:, 256:384], D3=TMPF[0][:, 384:512],
                     QT2=TMPB[1][:, 0:256], KT=TMPB[1][:, 256:384], X32=stg0b[:, 0:256], AM=stg0b[:, 256:512],
                     KHT=TMPB[1][:, 384:512], VIB=SQ[0][:, 0:512]),
            ]
            SS = [SST[0][:], RS[:, 0:128]]
            SBs = [RS[:, 128:384].bitcast(BF16)[:, j * 128:(j + 1) * 128] for j in range(4)]
            HS = [slice(0, 64), slice(64, 128)]
            P.barrier()
            for d in (range(2) if HGL >= 2 else ()):
                kc = 0
                DMA(SS[0], s0_d[l, d, hp, :, :], (), [("S", 0)], "S0")
                tiles = range(NT128) if d == 0 else range(NT128 - 1, -1, -1)
                for ts_ in range(2):
                    MEMSET(TSETS[ts_]["QT2"], 0.0, [("QT", ts_)])
                    MEMSET(TSETS[ts_]["X32"], 0.0, [("X3", ts_)])
                def prep(i, ts_, d=d):
                    tm = TSETS[ts_]
                    K_ = lambda n: (n, ts_)
                    cs = slice(i * 128, (i + 1) * 128)
                    tq = i // 4
                    CUM, E1, EX, D3 = tm["CUM"], tm["E1"], tm["EX"], tm["D3"]
                    QT2 = tm["QT2"].rearrange("p (h n) -> p h n", h=2)
                    X32 = tm["X32"].rearrange("p (h n) -> p h n", h=2)
                    X3 = tm["X32"][:, 0:128]
                    KT, AM, KHT, VIB = tm["KT"], tm["AM"], tm["KHT"], tm["VIB"]
                    P.op("dve", lambda e, CUM=CUM, cs=cs, d=d: e.tensor_tensor_scan(
                        out=CUM, data0=scanmask, data1=LG[d][:, cs], initial=0.0, op0=ALU.mult, op1=ALU.add),
                        [("LG", d, tq), "CF"], [K_("CUM")])
                    CUM3 = CUM.rearrange("p (c j) -> p c j", j=32)
                    TOTB = CUM3[:, :, 31:32].to_broadcast([128, 4, 32])
                    ACT(E1, CUM, AF.Exp, [K_("CUM")], [K_("E1")])
                    if d == 0:
                        for hh in range(2):
                            TT(QT2[HS[hh], hh, :], QS[HS[hh], cs], E1[HS[hh], :], ALU.mult, [("QS", tq), K_("E1")], [K_("QT")])
                        ACT(EX, CUM, AF.Exp, [K_("CUM")], [K_("EX")], scale=-1.0)
                        TT(KT, KG[0][:, cs], EX, ALU.mult, [("KG", 0, tq), K_("EX")], [K_("KT")])
                        TT(D3.rearrange("p (c j) -> p c j", j=32), TOTB, CUM3, ALU.subtract, [K_("CUM")], [K_("D3")])
                        ACT(D3, D3, AF.Exp, [K_("D3")], [K_("D3")])
                        TT(X3, KG[0][:, cs], D3, ALU.mult, [("KG", 0, tq), K_("D3")], [K_("X3")])
                        qint, qintk, ktr, ktrk = QT2, K_("QT"), X3, K_("X3")
                    else:
                        TT(D3.rearrange("p (c j) -> p c j", j=32), TOTB, CUM3, ALU.subtract, [K_("CUM")], [K_("D3")])
                        TT(D3, D3, LG[1][:, cs], ALU.add, [K_("D3"), ("LG", 1, tq)], [K_("D3")])
                        TT(CUM, CUM, LG[1][:, cs], ALU.subtract, [K_("CUM"), ("LG", 1, tq)], [K_("CUM")])
                        ACT(EX, CUM, AF.Exp, [K_("CUM")], [K_("EX")], scale=-1.0)
                        for hh in range(2):
                            TT(QT2[HS[hh], hh, :], QS[HS[hh], cs], EX[HS[hh], :], ALU.mult, [("QS", tq), K_("EX")], [K_("QT")])
                        ACT(EX, CUM, AF.Exp, [K_("CUM")], [K_("EX")])
                        TT(KT, KG[1][:, cs], EX, ALU.mult, [("KG", 1, tq), K_("EX")], [K_("KT")])
                        ACT(D3, D3, AF.Exp, [K_("D3")], [K_("D3")])
                        for hh in range(2):
                            TT(X32[HS[hh], hh, :], QS[HS[hh], cs], D3[HS[hh], :], ALU.mult, [("QS", tq), K_("D3")], [K_("X3")])
                        qint, qintk, ktr, ktrk = X32, K_("X3"), KT, K_("KT")
                    ba, bak = BANKS[6], ("bank", 6)
                    MM([(ba[:, hh * 128:(hh + 1) * 128], KT, QT2[:, hh, :], True, True)
                        for hh in range(2)], [K_("KT"), K_("QT")], [bak])
                    TT(AM.rearrange("p (h n) -> p h n", h=2), ba[:, 0:256].rearrange("p (h n) -> p h n", h=2),
                       trimask[d].unsqueeze(1).to_broadcast([128, 2, 128]), ALU.mult, [bak, "CB"], [K_("AM")])
                    bt_, btk = BANKS[7], ("bank", 7)
                    btb = bt_[:].bitcast(BF16)
                    TR([(btb[:, 0:128], ktr, ident_b)], [ktrk, "CB"], [btk])
                    ACT(KHT, btb[:, 0:128], AF.Copy, [btk], [K_("KHT")])
                    bu, buk = BANKS[3 * ts_], ("bank", 3 * ts_)
                    TT(VIB.rearrange("p (c n) -> p c n", c=4), VI[:, i, :].unsqueeze(1).to_broadcast([128, 4, 128]),
                       cmask.unsqueeze(2).to_broadcast([128, 4, 128]), ALU.mult, [("VI", i), "CF"], [K_("VIB")])
                    MM([(bu[:, 0:512], KHT, VIB, True, True)], [K_("KHT"), K_("VIB")], [buk])
                    bo = [(BANKS[3 * ts_ + 1 + hh], ("bank", 3 * ts_ + 1 + hh)) for hh in range(2)]
                    for hh in range(2):
                        MM([(bo[hh][0][:, 0:128], VI[:, i, :], AM[:, hh * 128:(hh + 1) * 128], True, False)],
                           [("VI", i), K_("AM")], [bo[hh][1]])
                    return dict(i=i, cs=cs, E1=E1, e1k=K_("E1"), qint=qint, qintk=qintk, bu=bu, buk=buk, bo=bo)

                def chain(cx, d=d):
                    nonlocal kc
                    i, cs, E1, qint, qintk, bu, buk, bo = (cx[k] for k in ("i", "cs", "E1", "qint", "qintk", "bu", "buk", "bo"))
                    chunks = range(4) if d == 0 else range(3, -1, -1)
                    for c in chunks:
                        Sc, Sn = SS[kc % 2], SS[(kc + 1) % 2]
                        sck, snk = ("S", kc % 2), ("S", (kc + 1) % 2)
                        Sb, sbk = SBs[kc % 4], ("Sb", kc % 4)
                        kc += 1
                        ACT(Sb, Sc, AF.Copy, [sck], [sbk])
                        last = (c == 3) if d == 0 else (c == 0)
                        for hh in range(2):
                            MM([(bo[hh][0][:, 32 * c:32 * c + 32], Sb, qint[:, hh, 32 * c:32 * c + 32], False, last)],
                               [sbk, qintk], [bo[hh][1]])
                        STT(Sn, Sc, E1[:, 32 * c + 31:32 * c + 32], bu[:, c * 128:(c + 1) * 128], ALU.mult, ALU.add,
                            [sck, cx["e1k"], buk], [snk])
                        tokpos = i * 128 + 32 * c
                        at_end = ((tokpos + 32) % 256 == 0) if d == 0 else (tokpos % 256 == 0)
                        if at_end:
                            seg = tokpos // 256
                            sg_ = SSTG[d]
                            ACT(sg_, Sn, AF.Copy, [snk], [("SSTG", 0)])
                            DMA(st_out[l, seg, d, hp, :, :], sg_, [("SSTG", 0)], [], "sstg0", is_out=True)
                            TS(Sn, Sn, vcol(V_CARRY), None, ALU.mult, None, [snk, "vecs"], [snk])
                    run_deferred(1, fixed_bank=7)
                    for hh in range(2):
                        rs_ = slice(hh * 64, (hh + 1) * 64)
                        if d == 0:
                            ACT(OA[rs_, cs], bo[hh][0][rs_, 0:128], AF.Copy, [bo[hh][1]], [("OA", i, hh)])
                        else:
                            TT(OA[rs_, cs], bo[hh][0][rs_, 0:128], OA[rs_, cs], ALU.add, [bo[hh][1], ("OA", i, hh)], [("OA", i, hh)])

                tl = list(tiles)
                cxs = {0: prep(tl[0], 0)}
                for n in range(len(tl)):
                    la, lb = [], []
                    if n + 1 < len(tl):
                        P.capture = la
                        cxs[n + 1] = prep(tl[n + 1], (n + 1) % 2)
                    P.capture = lb
                    chain(cxs.pop(n))
                    P.capture = None
                    ia = ib = 0
                    while ia < len(la) or ib < len(lb):
                        if ib < len(lb):
                            P.op(*lb[ib])
                            ib += 1
                        if ia < len(la):
                            P.op(*la[ia])
                            ia += 1
                P.barrier()
            for t in (range(NT) if HGL >= 6 else ()):
                oak = [("OA", i, hh) for i in range(4 * t, 4 * t + 4) for hh in range(2)]
                ACT(SQ[0][:], OA[:, tsl(t)], AF.Square, oak, [("SQ", 0)])
                bk, bkk = bank()
                MM([(bk[:], blockones, SQ[0][:], True, True)], [("SQ", 0), "CB"], [bkk])
                ACT(RS[:], bk[:], AF.Sqrt, [bkk, "epsv"], ["RS", "RSTD"], bias=SML[:, 0:1], scale=1.0 / 64)
                P.op("dve", lambda e: e.reciprocal(out=RSTD[:], in_=RS[:]), ["RS"], ["RSTD", "RS"])
                TT(TMPF[0][:], OA[:, tsl(t)], RSTD[:], ALU.mult, oak + ["RSTD"], [("TMPF", 0)])
                STT(Y[:, 2 + hp, tsl(t)], TMPF[0][:], vcol(VL + VL_HNG + hp), SG[:, tsl(t)], ALU.mult, ALU.mult,
                    [("TMPF", 0), "vecs", ("SG", t)], [("Y", 2 + hp, t)])

        if dbg:
            DMA(dbgY[l, :, :], Y[:].rearrange("p c t -> p (c t)"),
                [("Y", c) for c in range(2)] + [("Y", 6 + dc, i, gg) for dc in range(2) for i in range(NT128) for gg in range(2)]
                + [("Y", 4 + hp, m, hh) for hp in range(2) for m in range(NT128) for hh in range(2)]
                + [("Y", 2 + hp, t) for hp in range(2) for t in range(NT)], [], "dbgY", is_out=True)

        run_deferred(100)
        chk(l, 6)
        P.barrier()
        MGp, MAp = 0, 8
        MG = AR[:, MGp * T:(MGp + 8) * T].rearrange("p (c t) -> p c t", c=8)
        MACC = AR[:, MAp * T:(MAp + 4) * T].bitcast(F32).rearrange("p (c t) -> p c t", c=2)
        brn = ["w_br_a", "w_br_b", "w_br_c", "w_br_d"]

        def ykeys(br, k, t):
            c = 2 * br + k
            if br == 0:
                return [("Y", c)]
            if br == 1:
                return [("Y", c, t)]
            return [("Y", c, i, gg) for i in range(4 * t, 4 * t + 4) for gg in range(2)]
        for jp in range(4):
            for br in range(4):
                gw, gwk = wget("w_gate", l, 0, 8, br * 1024 + jp * 256, 256)
                bw, bwk = wget(brn[br], l, 0, 2, jp * 256, 256)
                for jj in range(2):
                    j = jp * 2 + jj
                    for t in range(NT):
                        b1, b1k = bank()
                        items, rd = [], [gwk]
                        for k in range(8):
                            r, rk = h_rhs(k, t)
                            items.append((b1[:], gw[:, k, jj * 128:(jj + 1) * 128], r, k == 0, k == 7))
                            rd += rk
                        MM(items, rd, [b1k])
                        gt = TMPB[(t + jj) % 2]
                        gtk = ("TMPB", (t + jj) % 2)
                        ACT(gt[:], b1[:], AF.Sigmoid, [b1k, "vecs"], [gtk], bias=vcol(VL + VL_BGATE + br * 8 + j))
                        b2, b2k = bank()
                        MM([(b2[:], bw[:, k, jj * 128:(jj + 1) * 128], Y[:, 2 * br + k, tsl(t)], k == 0, k == 1) for k in range(2)],
                           [bwk] + ykeys(br, 0, t) + ykeys(br, 1, t), [b2k])
                        mk = ("MACC", jj, t)
                        if br == 0:
                            TT(MACC[:, jj, tsl(t)], b2[:], gt[:], ALU.mult, [b2k, gtk], [mk])
                        else:
                            tf = TMPF[(t + jj) % 2]
                            tfk = ("TMPF", 0)
                            TT(tf[:], b2[:], gt[:], ALU.mult, [b2k, gtk], [tfk])
                            if br < 3:
                                TT(MACC[:, jj, tsl(t)], MACC[:, jj, tsl(t)], tf[:], ALU.add, [mk, tfk], [mk], eng="pool")
                            else:
                                TT(MG[:, j, tsl(t)], MACC[:, jj, tsl(t)], tf[:], ALU.add, [mk, tfk], [("MG", j, t)], eng="pool")

        def mg_rhs(k, t):
            return MG[:, k, tsl(t)], [("MG", k, t)]

        def evacO(oc, t, bk, bkk):
            STT(X[:, oc, tsl(t)], bk[:], modv[:, l * 48 + 16 + oc:l * 48 + 17 + oc], X[:, oc, tsl(t)], ALU.mult, ALU.add,
                [bkk, ("modv", l, 1), xkeys(oc, t)], [xkeys(oc, t)])
        lin_fm("w_o", l, 0, 8, 8, mg_rhs, NT, evacO)
        if dbg:
            DMA(dbgX[l, :, :], X[:].rearrange("p c t -> p (c t)"), [xkeys(c, t) for c in range(8) for t in range(NT)], [], "dbgX", is_out=True)

        chk(l, 7)
        P.barrier()
        norm_mod(l, 1)
        for c in range(8):
            hk = [("H", c, t) for t in range(NT)]
            MEMSET(H[:, c, 0, 0:1], 0.0, [("Hh", c)])
            MEMSET(H[:, c, NSEG - 1, 257:258], 0.0, [("Hh", c)])
            TS(H[:, c, 1:NSEG, 0:1], H[:, c, 0:NSEG - 1, 256:257], vcol(V_CARRY), None, ALU.mult, None, hk + ["vecs"], [("Hh", c)])
            TS(H[:, c, 0:NSEG - 1, 257:258], H[:, c, 1:NSEG, 1:2], vcol(V_CARRY), None, ALU.mult, None, hk + ["vecs"], [("Hh", c)])
        AV = AR[:, 0:11 * T].rearrange("p (c t) -> p c t", c=22)
        ZA = [AR[:, 11 * T + i * 1024:11 * T + (i + 1) * 1024].bitcast(F32).rearrange("p (s n) -> p s n", s=2) for i in range(2)]
        ZG = [TMPF[0][:].rearrange("p (s n) -> p s n", s=2), RS[:].rearrange("p (s n) -> p s n", s=2)]
        P.barrier()
        itf = 0
        for hf in range(2):
            for pi in range(22):
                wa, wak = wget("w_up", l, 0, 8, pi * 128, 128)
                wg, wgk = wget("w_up", l, 0, 8, DFF + pi * 128, 128)
                for s2 in range(2):
                    par = itf % 2
                    itf += 1
                    res = []
                    for (wv, wk, ch, zi) in ((wa, wak, pi, 0), (wg, wgk, 22 + pi, 1)):
                        b0 = 4 * par + 2 * zi
                        bkeys = [("bank", b0), ("bank", b0 + 1)]
                        for j in range(2):
                            seg = hf * 4 + s2 * 2 + j
                            rd = [("H", k, seg // 2) for k in range(8)] + [("Hh", k) for k in range(8)]
                            MM([(BANKS[b0 + j][:, 0:258], wv[:, k, :], H[:, k, seg, :], k == 0, k == 7) for k in range(8)], rd + [wk], [bkeys[j]])
                        PS2 = PSALL[:, b0 * 512:(b0 + 2) * 512].rearrange("p (s n) -> p s n", s=2)
                        z = (ZA if zi == 0 else ZG)[par]
                        zk = ("Z2", zi, par)
                        ACT(z, PS2[:, :, 1:257], AF.Identity, bkeys + ["vecs"], [zk], bias=vcol(VL + VL_FCB + ch), scale=vcol(VL + VL_FCW + 44 + ch))
                        STT(z, PS2[:, :, 0:256], vcol(VL + VL_FCW + ch), z, ALU.mult, ALU.add, bkeys + [zk, "vecs"], [zk])
                        STT(z, PS2[:, :, 2:258], vcol(VL + VL_FCW + 88 + ch), z, ALU.mult, ALU.add, bkeys + [zk, "vecs"], [zk])
                        res.append((z, zk))
                    sgb = TMPB[par][:, 0:512].rearrange("p (s n) -> p s n", s=2)
                    sgk = ("TMPB", par)
                    ACT(sgb, res[1][0], AF.Silu, [res[1][1]], [sgk])
                    TT(AV[:, pi, s2 * 512:(s2 + 1) * 512].rearrange("p (s n) -> p s n", s=2), sgb, res[0][0], ALU.mult,
                       [sgk, res[0][1]], [("AV", pi, s2)], eng="pool")
            for oc in range(8):
                w1, w1k = wget("w_down", l, 0, 11, oc * 128, 128)
                w2, w2k = wget("w_down", l, 11 * 128, 11, oc * 128, 128)
                for t2 in range(2):
                    t = hf * 2 + t2
                    bk, bkk = bank()
                    items, rd = [], [w1k, w2k]
                    for k in range(22):
                        wv = w1 if k < 11 else w2
                        items.append((bk[:], wv[:, k % 11, :], AV[:, k, t2 * 512:(t2 + 1) * 512], k == 0, k == 21))
                        rd.append(("AV", k, t2))
                    MM(items, rd, [bkk])
                    STT(X[:, oc, tsl(t)], bk[:], modv[:, l * 48 + 40 + oc:l * 48 + 41 + oc], X[:, oc, tsl(t)], ALU.mult, ALU.add,
                        [bkk, ("modv", l, 1), xkeys(oc, t)], [xkeys(oc, t)])
        if dbg and l == 1:
            DMA(dbgX[2, :, :], X[:].rearrange("p c t -> p (c t)"), [xkeys(c, t) for c in range(8) for t in range(NT)], [], "dbgX", is_out=True)

    try:
        for l in range(2):
            layer(l)
            chk(l, 8)
    except Stop:
        if dbg:
            P.barrier()
            DMA(dbgY[1, :, :], Y[:].rearrange("p c t -> p (c t)"), [], [], "dbgY", is_out=True)
            P.barrier()
            DMA(dbgX[2, :, :], X[:].rearrange("p c t -> p (c t)"), [], [], "dbgX", is_out=True)

    P.barrier()
    YT = AR[:, 0:8 * 512 * 2].bitcast(F32).rearrange("p (c n) -> p c n", c=8)
    OST = [AR[:, 8192 + j * 2048:8192 + (j + 1) * 2048].bitcast(F32) for j in range(4)]
    for t in range(NT):
        rs, rsk = stats(t, final=True)
        for c in range(8):
            STT(YT[:, c, :], X[:, c, tsl(t)], vcol(V_FNG + c), rs, ALU.mult, ALU.mult, [xkeys(c, t), "vecs", rsk], [("YT", c)])
        for i4 in range(4):
            i = t * 4 + i4
            ost, ostk = OST[i % 4], ("ost", i % 4)
            for hf in range(2):
                bk, bkk = bank()
                TR([(bk[:, j * 128:(j + 1) * 128], YT[:, hf * 4 + j, i4 * 128:(i4 + 1) * 128], ident_f) for j in range(4)],
                   [("YT", hf * 4 + j) for j in range(4)] + ["CF"], [bkk])
                if hf == 0:
                    ACT(ost[:, 0:512], bk[:], AF.Copy, [bkk], [ostk])
                else:
                    P.op("dve", lambda e, bk=bk, ost=ost: e.tensor_copy(out=ost[:, 512:1024], in_=bk[:]), [bkk], [ostk])
            DMA(y_out[i * 128:(i + 1) * 128, :], ost, [ostk], [], "ost%d" % (i % 4), is_out=True)
    return P, rec


V_CVEC = 0
V_CARRY = 8
V_CTXB = 9
V_HLB = 10
V_FNG = 18
VL_N1G, VL_N2G, VL_BADA, VL_CAW, VL_CAB, VL_HNG, VL_BGATE, VL_FCW, VL_FCB = 0, 8, 16, 64, 70, 72, 74, 106, 238
VL_SIZE = 282
V_L = [26, 26 + VL_SIZE]
NV = 26 + 2 * VL_SIZE


def colv(v):
    v = np.asarray(v, np.float32).reshape(-1, 128)
    return np.ascontiguousarray(v.T)


_CACHE = {}


def get_nc(dbg=False, upto=99):
    key = ("nc", dbg, upto)
    if key not in _CACHE:
        _, rec = build(None, dbg, upto)
        P, _ = build(rec, dbg, upto)
        _CACHE[key] = P.finish()
    return _CACHE[key]


def host_prep(inputs, core):
    f = lambda a: np.ascontiguousarray(np.asarray(a, np.float32))
    prompt = core < 4
    m = {}
    if prompt:
        m["x_in"] = f(inputs["x_prompt"][core * 8:(core + 1) * 8].reshape(T, D))
        cvec = inputs["c_ctx"]
    else:
        b = core - 4
        m["x_in"] = f(inputs["x_sample"][b])
        cvec = inputs["c"][b]
    vecs = np.zeros((128, NV), np.float32)
    vecs[:, V_CVEC:V_CVEC + 8] = colv(cvec)
    vecs[:, V_CARRY] = 0.0 if prompt else 1.0
    vecs[:, V_CTXB] = NEG if prompt else 0.0
    hl = np.asarray(inputs["hgrn_lb"], np.float32)
    vecs[:, V_HLB:V_HLB + 4] = colv(hl[0])
    vecs[:, V_HLB + 4:V_HLB + 8] = colv(hl[1])
    vecs[:, V_FNG:V_FNG + 8] = colv(inputs["final_norm_g"])
    for l in range(2):
        o = V_L[l]
        vecs[:, o + VL_N1G:o + VL_N1G + 8] = colv(inputs["norm1_g"][l])
        vecs[:, o + VL_N2G:o + VL_N2G + 8] = colv(inputs["norm2_g"][l])
        vecs[:, o + VL_BADA:o + VL_BADA + 48] = colv(inputs["b_ada"][l])
        vecs[:, o + VL_CAW:o + VL_CAW + 6] = colv(inputs["conv_a_w"][l])
        vecs[:, o + VL_CAB:o + VL_CAB + 2] = colv(inputs["conv_a_b"][l])
        vecs[:, o + VL_HNG:o + VL_HNG + 2] = colv(inputs["hgrn_norm_g"][l])
        vecs[:, o + VL_BGATE:o + VL_BGATE + 32] = colv(inputs["b_gate"][l])
        vecs[:, o + VL_FCW:o + VL_FCW + 132] = colv(inputs["ffn_conv_w"][l])
        vecs[:, o + VL_FCB:o + VL_FCB + 44] = colv(inputs["ffn_conv_b"][l])
    m["vecs"] = vecs
    ii = np.arange(128)
    ident = np.eye(128, dtype=np.float32)
    blockmask = ((ii[:, None] // 64) == (ii[None, :] // 64)).astype(np.float32)
    scanmask = np.broadcast_to(((ii % 32) != 0).astype(np.float32)[None, :], (128, 128))
    cmask = (ii[:, None] // 32 == np.arange(4)[None, :]).astype(np.float32)
    m["c_f32"] = f(np.concatenate([ident, blockmask, scanmask, cmask], axis=1))
    same = (ii[:, None] // 32) == (ii[None, :] // 32)
    trif = (same & (ii[:, None] <= ii[None, :])).astype(np.float32)
    trib = (same & (ii[:, None] >= ii[None, :])).astype(np.float32)
    m["c_bf"] = f(np.concatenate([ident, np.ones((128, 128), np.float32), blockmask, trif, trib], axis=1))
    m["gng_rep"] = f(np.broadcast_to(np.asarray(inputs["gmlp_norm_g"], np.float32)[:, None, :], (2, 128, 256)))
    m["gmlp_brow"] = f(np.asarray(inputs["gmlp_b"], np.float32).reshape(2, 1, 512))
    ws = np.asarray(inputs["gmlp_ws"], np.float32)
    m["gmlp_wsT"] = f(ws.transpose(0, 3, 1, 2).reshape(2, 128, 512))
    s0 = np.zeros((2, 2, 2, 128, 128), np.float32)
    ctxk = np.zeros((2, 512, 256), np.float32)
    ctxv = np.zeros((2, 512, 256), np.float32)
    if not prompt:
        st = np.asarray(inputs["state_hgrn"], np.float32)[core - 4]
        for hp in range(2):
            for hh in range(2):
                s0[:, :, hp, hh * 64:(hh + 1) * 64, hh * 64:(hh + 1) * 64] = st[:, :, 2 * hp + hh]
        ck = np.asarray(inputs["cache_k"], np.float32)[core - 4]
        cv = np.asarray(inputs["cache_v"], np.float32)[core - 4]
        ctxk = f(ck.transpose(0, 2, 1, 3).reshape(2, 512, 256))
        ctxv = f(cv.transpose(0, 2, 1, 3).reshape(2, 512, 256))
    m["s0"] = s0
    m["ctxk"] = ctxk
    m["ctxv"] = ctxv
    rpb = np.asarray(inputs["na_rpb"], np.float32)
    nab = np.full((2, 2, 128, NCLS, 2, 5, 128), NEG, np.float32)
    rep = {0: 0, 1: 1, 2: 2, 3: 3, 4: 14, 5: 15}
    kk = np.arange(128)
    krow_l, kcol = kk // 64, kk % 64
    qrow_l, qcol = kk // 64, kk % 64
    for cls, mrep in rep.items():
        t0 = tile0_of(mrep)
        for j in range(5):
            krow = 2 * (t0 + j) + krow_l
            qrow = 2 * mrep + qrow_l
            if prompt:
                ok = (krow[:, None] // 4) == (qrow[None, :] // 4)
                val = np.where(ok, 0.0, NEG).astype(np.float32)
                nab[:, :, :, cls, :, j, :] = val[None, None, :, None, :]
            else:
                kr0 = np.clip(qrow - 4, 0, 24)
                rowok = (krow[:, None] >= kr0[None, :]) & (krow[:, None] < kr0[None, :] + 8)
                ws_ = np.clip(qcol - 8, 0, 48)
                colok = (kcol[:, None] >= ws_[None, :]) & (kcol[:, None] < ws_[None, :] + 16)
                ok = rowok & colok
                drow = np.clip(krow[:, None] - qrow[None, :] + 7, 0, 14)
                dcol = np.clip(kcol[:, None] - qcol[None, :] + 15, 0, 30)
                for l in range(2):
                    for hp in range(2):
                        for hh in range(2):
                            g = rpb[l, 2 * hp + hh][drow, dcol]
                            nab[l, hp, :, cls, hh, j, :] = np.where(ok, g, NEG)
    m["nabias"] = f(nab.reshape(2, 2, 128, NCLS * 2 * 5 * 128))
    return m


WNAMES = ("w_ada", "w_in", "w_gate", "w_br_a", "w_br_b", "w_br_c", "w_br_d", "w_o", "w_up", "w_down")


def kernel(**inputs):
    dbg = bool(inputs.pop("_dbg", False))
    upto = int(inputs.pop("_upto", 99))
    nc = get_nc(dbg, upto)
    wts = {nm: np.ascontiguousarray(np.asarray(inputs[nm], np.float32)) for nm in WNAMES}
    in_maps = []
    for core in range(8):
        m = host_prep(inputs, core)
        m.update(wts)
        in_maps.append(m)
    res = run_bass_kernel_spmd(nc, in_maps, core_ids=list(range(8)))
    R = res.results
    y_prompt = np.concatenate([R[c]["y_out"].reshape(8, 256, D) for c in range(4)], axis=0)
    y_sample = np.stack([R[c]["y_out"] for c in range(4, 8)], axis=0)
    def cache(name):
        outs = []
        for c in range(4):
            a = R[c][name].reshape(2, 8, 256, 4, 64)
            outs.append(a.transpose(1, 0, 3, 2, 4))
        return np.ascontiguousarray(np.concatenate(outs, axis=0))
    nk, nv = cache("kc_out"), cache("vc_out")
    sts = []
    for c in range(4):
        a = R[c]["st_out"]
        o = np.zeros((8, 2, 2, 4, 64, 64), np.float32)
        for hp in range(2):
            for hh in range(2):
                blk = a[:, :, :, hp, hh * 64:(hh + 1) * 64, hh * 64:(hh + 1) * 64]
                o[:, :, :, 2 * hp + hh] = blk.transpose(1, 0, 2, 3, 4)
        sts.append(o)
    ns = np.concatenate(sts, axis=0)
    outs = (np.ascontiguousarray(y_prompt, dtype=np.float32), np.ascontiguousarray(y_sample, dtype=np.float32),
            nk.astype(np.float32), nv.astype(np.float32), ns.astype(np.float32))
    if dbg:
        return outs, R
    return outs
```
